# Optimizing a Trainium2 kernel written in Bass

```python
import math, functools
import jax, jax.numpy as jnp
from jax import lax
import numpy as np

D_MODEL = 1024
BATCH = 1
SEQ = 16384
DEPTH = 2
DEC_BATCH = 16
DEC_SEQ = 64
PAST_LEN = 2048

CHUNK = 64
HEAD_DIM = 64
N_HEADS_A = 16
N_KV_A = 2
WINDOW_A = 128
PAST_CHUNKS_A = WINDOW_A // CHUNK
N_HEADS_B = 16
PAST_CHUNKS_B = 8
REL_CLIP_B = 128
T5_BUCKETS = 32
T5_MAX_DIST = 128
WIDTH_A = N_HEADS_A * HEAD_DIM
WIDTH_B = N_HEADS_B * HEAD_DIM
IN_A = 2 * WIDTH_A + 2 * N_KV_A * HEAD_DIM
IN_B = 2 * WIDTH_B + 2 * N_HEADS_B * HEAD_DIM
EPS = 1e-6
NEG_INF = -1e30

kernel_name = 'hybrid_streaming_swa_sink_chunkband_step'


def rms_norm(x, g):
    xf = x.astype(jnp.float32)
    y = xf * lax.rsqrt(jnp.mean(xf * xf, axis=-1, keepdims=True) + EPS)
    return (y * g.astype(jnp.float32)).astype(x.dtype)


def t5_bucket(rel):
    nb = T5_BUCKETS // 2
    max_exact = nb // 2
    ret = jnp.where(rel > 0, nb, 0)
    n = jnp.abs(rel)
    nf = jnp.maximum(n, 1).astype(jnp.float32)
    large = max_exact + (jnp.log(nf / max_exact) / math.log(T5_MAX_DIST / max_exact)
                         * (nb - max_exact)).astype(jnp.int32)
    large = jnp.minimum(large, nb - 1)
    return ret + jnp.where(n < max_exact, n, large)


def t5_bias(table, q_pos, k_pos):
    b = table[t5_bucket(k_pos[None, :] - q_pos[:, None])]
    return jnp.transpose(b, (2, 0, 1)).astype(jnp.float32)


def clipped_rel_bias(table, q_pos, k_pos):
    idx = jnp.clip(q_pos[:, None] - k_pos[None, :], -REL_CLIP_B, REL_CLIP_B) + REL_CLIP_B
    return table[:, idx].astype(jnp.float32)


def attend(q, k, v, bias, valid, sinks):
    b, tq, h, dh = q.shape
    kv = k.shape[2]
    g = h // kv
    tk = k.shape[1]
    qg = q.reshape(b, tq, kv, g, dh)
    s = jnp.einsum('bqkgd,bskd->bkgqs', qg, k).astype(jnp.float32) * (dh ** -0.5)
    s = s + bias.reshape(kv, g, tq, tk)
    if valid is not None:
        s = jnp.where(valid, s, NEG_INF)
    if sinks is not None:
        sink = sinks.astype(jnp.float32).reshape(kv, g, 1, 1)
        m = jnp.maximum(jnp.max(s, axis=-1, keepdims=True), sink)
        p = jnp.exp(s - m)
        p = p / (jnp.sum(p, axis=-1, keepdims=True) + jnp.exp(sink - m))
    else:
        p = jax.nn.softmax(s, axis=-1)
    o = jnp.einsum('bkgqs,bskd->bqkgd', p.astype(v.dtype), v)
    return o.reshape(b, tq, h, dh)


def project(x, norm_g, w_in, q_g, k_g, n_heads, n_kv):
    h = rms_norm(x, norm_g)
    p = h @ w_in
    wq = n_heads * HEAD_DIM
    wkv = n_kv * HEAD_DIM
    q, k, v, gate = jnp.split(p, [wq, wq + wkv, wq + 2 * wkv], axis=-1)
    lead = x.shape[:-1]
    q = rms_norm(q.reshape(*lead, n_heads, HEAD_DIM), q_g)
    k = rms_norm(k.reshape(*lead, n_kv, HEAD_DIM), k_g)
    v = v.reshape(*lead, n_kv, HEAD_DIM)
    return q, k, v, gate


def prompt_band_attention(q, k, v, past_chunks, bias_fn, sinks):
    b, s, h, dh = q.shape
    nc = s // CHUNK
    pad = past_chunks * CHUNK
    band = pad + CHUNK
    kp = jnp.pad(k, ((0, 0), (pad, 0), (0, 0), (0, 0)))
    vp = jnp.pad(v, ((0, 0), (pad, 0), (0, 0), (0, 0)))
    q_pos = pad + jnp.arange(CHUNK, dtype=jnp.int32)
    k_pos = jnp.arange(band, dtype=jnp.int32)
    bias = bias_fn(q_pos, k_pos)

    def one_chunk(c):
        start = c * CHUNK
        qc = lax.dynamic_slice_in_dim(q, start, CHUNK, axis=1)
        kc = lax.dynamic_slice_in_dim(kp, start, band, axis=1)
        vc = lax.dynamic_slice_in_dim(vp, start, band, axis=1)
        valid = (start + k_pos) >= pad
        return attend(qc, kc, vc, bias, valid, sinks)

    o = lax.map(one_chunk, jnp.arange(nc, dtype=jnp.int32))
    return jnp.transpose(o, (1, 0, 2, 3, 4)).reshape(b, s, h * dh)


def sample_band_attention(q, k_new, v_new, k_cache, v_cache, bias_fn, sinks):
    b, tn, h, dh = q.shape
    L = k_cache.shape[1]
    kb = jnp.concatenate([k_cache, k_new], axis=1)
    vb = jnp.concatenate([v_cache, v_new], axis=1)
    q_pos = L + jnp.arange(tn, dtype=jnp.int32)
    k_pos = jnp.arange(L + tn, dtype=jnp.int32)
    o = attend(q, kb, vb, bias_fn(q_pos, k_pos), None, sinks)
    return o.reshape(b, tn, h * dh), kb[:, tn:], vb[:, tn:]


def gated_attention_layer(x_p, x_s, ck, cv, norm_g, w_in, q_g, k_g, w_out,
                          n_heads, n_kv, past_chunks, bias_fn, sinks):
    qp, kp, vp, gp = project(x_p, norm_g, w_in, q_g, k_g, n_heads, n_kv)
    op = prompt_band_attention(qp, kp, vp, past_chunks, bias_fn, sinks)
    y_p = x_p + (op * jax.nn.silu(gp)) @ w_out
    qs, ks, vs, gs = project(x_s, norm_g, w_in, q_g, k_g, n_heads, n_kv)
    os_, nk, nv = sample_band_attention(qs, ks, vs, ck, cv, bias_fn, sinks)
    y_s = x_s + (os_ * jax.nn.silu(gs)) @ w_out
    keep = past_chunks * CHUNK
    return y_p, y_s, kp[:, -keep:], vp[:, -keep:], nk, nv


def setup_inputs(seed: int = 0) -> dict:
    key = jax.random.key(seed)
    ks = jax.random.split(key, 20)
    f32 = jnp.float32
    la = min(WINDOW_A, PAST_LEN)
    lb = min(PAST_CHUNKS_B * CHUNK, PAST_LEN)
    nrm = lambda k, shape, s=1.0: (jax.random.normal(k, shape, f32) * s).astype(f32)
    return {
        'x_prompt': nrm(ks[0], (BATCH, SEQ, D_MODEL)),
        'x_sample': nrm(ks[1], (DEC_BATCH, DEC_SEQ, D_MODEL)),
        'cache_k_a': nrm(ks[2], (DEC_BATCH, la, N_KV_A, HEAD_DIM)),
        'cache_v_a': nrm(ks[3], (DEC_BATCH, la, N_KV_A, HEAD_DIM)),
        'cache_k_b': nrm(ks[4], (DEC_BATCH, lb, N_HEADS_B, HEAD_DIM)),
        'cache_v_b': nrm(ks[5], (DEC_BATCH, lb, N_HEADS_B, HEAD_DIM)),
        't5_table': nrm(ks[6], (T5_BUCKETS, N_HEADS_A), 0.3),
        'norm_a': 1.0 + nrm(ks[7], (D_MODEL,), 0.02),
        'w_in_a': nrm(ks[8], (D_MODEL, IN_A), D_MODEL ** -0.5),
        'q_norm_a': 1.0 + nrm(ks[9], (HEAD_DIM,), 0.02),
        'k_norm_a': 1.0 + nrm(ks[10], (HEAD_DIM,), 0.02),
        'sinks_a': nrm(ks[11], (N_HEADS_A,), 0.5),
        'w_out_a': nrm(ks[12], (WIDTH_A, D_MODEL), WIDTH_A ** -0.5),
        'norm_b': 1.0 + nrm(ks[13], (D_MODEL,), 0.02),
        'w_in_b': nrm(ks[14], (D_MODEL, IN_B), D_MODEL ** -0.5),
        'q_norm_b': 1.0 + nrm(ks[15], (HEAD_DIM,), 0.02),
        'k_norm_b': 1.0 + nrm(ks[16], (HEAD_DIM,), 0.02),
        'rel_bias_b': nrm(ks[17], (N_HEADS_B, 2 * REL_CLIP_B + 1), 0.3),
        'w_out_b': nrm(ks[18], (WIDTH_B, D_MODEL), WIDTH_B ** -0.5),
    }


def reference(x_prompt, x_sample, cache_k_a, cache_v_a, cache_k_b, cache_v_b,
              t5_table, norm_a, w_in_a, q_norm_a, k_norm_a, sinks_a, w_out_a,
              norm_b, w_in_b, q_norm_b, k_norm_b, rel_bias_b, w_out_b):
    bias_a = functools.partial(t5_bias, t5_table)
    bias_b = functools.partial(clipped_rel_bias, rel_bias_b)
    x_p, x_s = x_prompt, x_sample
    for i in range(DEPTH):
        if i % 2 == 0:
            x_p, x_s, k_a_p, v_a_p, k_a_s, v_a_s = gated_attention_layer(
                x_p, x_s, cache_k_a, cache_v_a, norm_a, w_in_a, q_norm_a, k_norm_a,
                w_out_a, N_HEADS_A, N_KV_A, PAST_CHUNKS_A, bias_a, sinks_a)
        else:
            x_p, x_s, k_b_p, v_b_p, k_b_s, v_b_s = gated_attention_layer(
                x_p, x_s, cache_k_b, cache_v_b, norm_b, w_in_b, q_norm_b, k_norm_b,
                w_out_b, N_HEADS_B, N_HEADS_B, PAST_CHUNKS_B, bias_b, None)
    return (x_p, x_s, k_a_p, v_a_p, k_b_p, v_b_p, k_a_s, v_a_s, k_b_s, v_b_s)
```

```python
import numpy as np
import concourse.bass as bass
import concourse.mybir as mybir
from concourse.bass_utils import run_bass_kernel_spmd

F32 = mybir.dt.float32
BF16 = mybir.dt.bfloat16
AF = mybir.ActivationFunctionType
ALU = mybir.AluOpType
AX = mybir.AxisListType

NCORES = 8
import os as _os0
PIPELINE = _os0.environ.get("KPIPE", "1") == "1"
GREEDY = _os0.environ.get("KGREEDY", "1") == "1"
LANE_BIAS = float(_os0.environ.get("KBIAS", "-450"))
HALF = _os0.environ.get("KHALF", "1") == "1"
NPA = int(_os0.environ.get("KNPA", "3"))
NPTR = int(_os0.environ.get("KNPTR", "1"))
NPSB = int(_os0.environ.get("KNPSB", "3"))
POOL = _os0.environ.get("KPOOL", "0") == "1"
DAGSCHED = _os0.environ.get("KDAG", "1") == "1"
SCHED_Q = float(_os0.environ.get("KQ", "300"))
PRIO = _os0.environ.get("KPRIO", "blevel")
NQS = int(_os0.environ.get("KNQS", "1"))
MAXPROJ = int(_os0.environ.get("KMAXPROJ", "4"))
DMA_T = _os0.environ.get("KDMAT", "0") == "1"
INPROJ_ORDER = _os0.environ.get("KORDER", "N Pq Cq Tq Pk Ck Tk Pv Cv Pg Cg").split()
D = 1024
NH = 5
NM = 16
EPS = 1e-6
ONESV = 2.0
MASKV = -30000.0
IN_A = 2304
IN_B = 4096


class _Op:
    __slots__ = ("eng", "fn", "deps", "idx", "dsem", "token", "needed")


class Sched:
    def __init__(self):
        self.ops = []
        self.lastw = {}
        self.readers = {}
        self.cur = []

    EXCL = ("pa", "psb", "ptr", "po", "bk", "vb", "pb")

    def add(self, eng, fn, reads=(), writes=(), dsem=None):
        ex = [r for r in reads if (r[0] if isinstance(r, tuple) else r) in self.EXCL]
        if ex:
            reads = [r for r in reads if r not in ex]
            writes = list(writes) + [r for r in ex if r not in writes]
        self.cur.append((eng, fn, list(reads), list(writes), dsem))

    def resolve(self, order):
        for (eng, fn, reads, writes, dsem) in order:
            deps = set()
            for r in reads:
                if r in self.lastw:
                    deps.add(self.lastw[r])
            for w in writes:
                if w in self.lastw:
                    deps.add(self.lastw[w])
                for q in self.readers.get(w, ()):
                    deps.add(q)
            op = _Op()
            op.eng, op.fn, op.deps, op.dsem = eng, fn, deps, dsem
            op.idx = len(self.ops)
            op.token = None
            op.needed = False
            deps.discard(op.idx)
            for r in reads:
                self.readers.setdefault(r, []).append(op.idx)
            for w in writes:
                self.lastw[w] = op.idx
                self.readers[w] = []
            self.ops.append(op)

    def finalize(self, engsems, group_sems=(), burst_sems=()):
        for op in self.ops:
            for d in op.deps:
                self.ops[d].needed = True
        cnt = {}
        tot = {}
        for op in self.ops:
            if op.dsem is not None:
                tot[op.dsem] = tot.get(op.dsem, 0) + 16
        for op in self.ops:
            if op.dsem is not None:
                cnt[op.dsem] = cnt.get(op.dsem, 0) + 16
                if op.dsem in group_sems:
                    v = tot[op.dsem]
                elif op.dsem in burst_sems:
                    v = ((cnt[op.dsem] + 127) // 128) * 128
                else:
                    v = cnt[op.dsem]
                op.token = (op.dsem, v)
            elif op.needed:
                s = engsems[op.eng]
                cnt[s] = cnt.get(s, 0) + 1
                op.token = (s, cnt[s])
        self.final = cnt

    def emit(self, engname, eng, extra_final=()):
        waited = {}
        for op in self.ops:
            if op.eng != engname:
                continue
            need = {}
            for d in op.deps:
                dop = self.ops[d]
                if dop.eng == "pe" and engname == "pe" and dop.dsem is None:
                    continue
                s, v = dop.token
                if need.get(s, 0) < v:
                    need[s] = v
            for s, v in need.items():
                if waited.get(s, 0) < v:
                    eng.wait_ge(s, v)
                    waited[s] = v
            ins = op.fn(eng)
            if op.token is not None:
                ins.then_inc(op.token[0], 16 if op.dsem is not None else 1)
        for s, v in extra_final:
            eng.wait_ge(s, v)


class _FakeIns:
    def then_inc(self, *a, **k):
        return self


class _FakeEng:
    def __init__(self, engname):
        self.engname = engname
        self.t = 0.0
        self.dma = False

    def __getattr__(self, name):
        def f(*args, **kw):
            out = kw.get("out", args[0] if args else None)
            n = 64
            if out is not None and hasattr(out, "shape"):
                n = 1
                for d in list(out.shape)[1:]:
                    n *= int(d)
            if name == "dma_start":
                self.dma = True
                self.t += 2500.0 + 2.0 * n
            elif name in ("matmul", "transpose"):
                self.t += 22.0 + 0.42 * n + 0.13 * max(0, n - 256)
            elif self.engname == "act":
                self.t += 120.0 + 0.78 * n
            elif self.engname == "pool":
                self.t += 1260.0 if n <= 64 else 100.0 + 2.0 * n
            else:
                self.t += 100.0 + 0.72 * n
            return _FakeIns()
        return f


def _op_cost(rec):
    fe = _FakeEng(rec[0])
    rec[1](fe)
    return fe.t, fe.dma


class _Timeline:
    SYNC = 250.0

    def __init__(self):
        self.eng_free = {}
        self.w_t = {}
        self.r_t = {}
        self.cost = {}
        self.busy = {}

    def _cost(self, rec):
        k = id(rec[1])
        if k not in self.cost:
            self.cost[k] = _op_cost(rec)
        return self.cost[k]

    def start_time(self, rec):
        eng, fn, reads, writes, dsem = rec
        t = self.eng_free.get(eng, 0.0)
        for r in reads:
            if r in self.w_t:
                ft, fe = self.w_t[r]
                t = max(t, ft + (self.SYNC if fe != eng else 0.0))
        for w in writes:
            if w in self.w_t:
                ft, fe = self.w_t[w]
                t = max(t, ft + (self.SYNC if fe != eng else 0.0))
            for (ft, fe) in self.r_t.get(w, ()):
                t = max(t, ft + (self.SYNC if fe != eng else 0.0))
        return t

    def commit(self, rec):
        eng, fn, reads, writes, dsem = rec
        st = self.start_time(rec)
        c, is_dma = self._cost(rec)
        if is_dma:
            self.eng_free[eng] = st + (900.0 if eng == "pool" else 100.0)
            fin, fe = st + c, "dma"
        else:
            self.eng_free[eng] = st + c
            fin, fe = st + c, eng
            self.busy[eng] = self.busy.get(eng, 0.0) + c
        for r in reads:
            self.r_t.setdefault(r, []).append((fin, fe))
        for w in writes:
            self.w_t[w] = (fin, fe)
            self.r_t[w] = []
        return fin

def _t5_bucket(rel):
    nb = 16
    max_exact = 8
    ret = np.where(rel > 0, nb, 0)
    n = np.abs(rel)
    nf = np.maximum(n, 1).astype(np.float32)
    large = max_exact + (np.log(nf / max_exact) / np.float32(np.log(128 / max_exact))
                         * (nb - max_exact)).astype(np.int32)
    large = np.minimum(large, nb - 1)
    return ret + np.where(n < max_exact, n, large)


def _consts():
    n = np.arange(256)
    relA = 63 - n
    bA = _t5_bucket(relA.astype(np.int32))
    oha = np.zeros((32, 256), np.float32)
    oha[bA, n] = 1.0
    relB = n - 63
    idxB = np.clip(relB, -128, 128) + 128
    ohb = np.zeros((384, 256), np.float32)
    ohb[idxB, n] += 1.0
    ohb[256, :] -= 1.0
    ident = np.eye(128, dtype=np.float32)
    adm = np.zeros((128, 384), np.float32)
    r = np.arange(128)
    adm[r, 255 - r] = 1.0
    return oha, ohb.reshape(3, 128, 256), ident, adm


def build(nm=NM, do_sample=True):
    nc = bass.Bass("TRN2", target_bir_lowering=False)
    NU = NH + nm

    def din(name, shape):
        return nc.dram_tensor(name, list(shape), F32, kind="ExternalInput").ap()

    def dout(name, shape):
        return nc.dram_tensor(name, list(shape), F32, kind="ExternalOutput").ap()

    xin = din("xin", [NU * 128, D])
    xs_in = din("xs", [128, D])
    cka = din("cka", [2, 128, 128])
    cva = din("cva", [2, 128, 128])
    ckb = din("ckb", [2, 512, 1024])
    cvb = din("cvb", [2, 512, 1024])
    t5 = din("t5", [32, 16])
    rbT = din("rbT", [3, 128, 16])
    oha_d = din("oha", [32, 256])
    ohb_d = din("ohb", [3, 128, 256])
    ident_d = din("ident", [128, 128])
    adm_d = din("adm", [128, 384])
    maskv_d = din("maskv", [128, 1])
    norm_a = din("norm_a", [D])
    norm_b = din("norm_b", [D])
    qn_a = din("q_norm_a", [64])
    kn_a = din("k_norm_a", [64])
    qn_b = din("q_norm_b", [64])
    kn_b = din("k_norm_b", [64])
    sinks = din("sinks", [16])
    w_in_a = din("w_in_a", [D, IN_A])
    w_out_a = din("w_out_a", [D, D])
    w_in_b = din("w_in_b", [D, IN_B])
    w_out_b = din("w_out_b", [D, D])

    y_p = dout("y_p", [nm * 128, D])
    y_s = dout("y_s", [128, D])
    kap = dout("kap", [128, 128])
    vap = dout("vap", [128, 128])
    kbp = dout("kbp", [512, 1024])
    vbp = dout("vbp", [512, 1024])
    kas = dout("kas", [2, 128, 128])
    vas = dout("vas", [2, 128, 128])
    kbs = dout("kbs", [2, 512, 1024])
    vbs = dout("vbs", [2, 512, 1024])
    vv_d = nc.dram_tensor("vv_scratch", [2, 16, 256], F32, kind="Internal").ap()

    S = Sched()
    ctx = []
    marks = {}

    def sb(name, shape, dt):
        cm = nc.sbuf_tensor(name, list(shape), dt)
        t = cm.__enter__()
        ctx.append(cm)
        return t

    def ps(name, shape, dt):
        cm = nc.psum_tensor(name, list(shape), dt)
        t = cm.__enter__()
        ctx.append(cm)
        return t

    def sem(name):
        cm = nc.semaphore(name)
        t = cm.__enter__()
        ctx.append(cm)
        return t

    WA = sb("WA", [128, 8, IN_A], BF16)
    WOA = sb("WOA", [128, 8, D], BF16)
    WB = sb("WB", [128, 8, IN_B], BF16)
    WOB = sb("WOB", [128, 8, D], BF16)
    XS = [sb(f"x{i}", [128, D], F32) for i in range(2)]
    NXB = int(_os0.environ.get("KNXB", "2"))
    XNS = [sb(f"xn{i}", [128, D], BF16) for i in range(NXB)]
    XTS = [sb(f"xT{i}", [128, 8, 128], BF16) for i in range(NXB)]
    XN, XT = XNS[0], XTS[0]
    SQ = sb("sq", [128, D], F32)
    QN = sb("qn", [128, D], BF16)
    QTS = [sb(f"qTbd{i}", [128, 8, 2, 128], BF16) for i in range(NQS)]
    NKB = 6
    KTB = [sb(f"kTB{i}", [128, 8, 128], BF16) for i in range(NKB)]
    VB = [sb(f"VB{i}", [128, 16, 65], BF16) for i in range(NKB)]
    KTA = [sb(f"kTA{i}", [128, 2, 128], BF16) for i in range(3)]
    VA = [sb(f"VA{i}", [128, 2, 65], BF16) for i in range(3)]
    PT = sb("PT", [128, 5, 512], BF16)
    GAS = [sb(f"ga{i}", [128, D], BF16) for i in range(NQS)]
    OB = sb("ob", [128, D], BF16)
    OGT = sb("ogT", [128, 8, 128], BF16)
    DN1 = [sb(f"dn1{l}", [128, 16, 64], BF16) for l in range(2)]
    DN2M = sb("dn2m", [128, 16, 64], BF16)
    IDENT = sb("identb", [128, 128], BF16)
    ADM = sb("admb", [128, 384], BF16)
    SM = sb("small", [128, 384], F32)
    GKBC = [sb(f"gkbc{l}", [128, 64], F32) for l in range(2)]
    ESBC = sb("esbc", [128, 16], F32)
    VVS = XS[1][0:16, 0:512].rearrange("p (l n) -> p l n", l=2)
    T5S = XS[1][0:32, 512:528]
    RBS = XS[1][:, 528:576].rearrange("p (k h) -> p k h", k=3)
    GQBC = [XS[1][:, 576 + 64 * l:640 + 64 * l] for l in range(2)]
    OHBS = SQ[:, 0:768].rearrange("p (k n) -> p k n", k=3)
    OHAS = SQ[0:32, 768:1024]

    C_SS = 0
    C_RX = 1
    C_SSQ = 8
    C_RQ = 40
    C_DEN = 72
    C_RDEN = 76
    C_MHALF = 80
    C_ZERO = 112
    C_MASK = 113
    C_GQK = [114, 115]
    C_GQ = [116, 117]
    C_GK = [118, 119]
    C_GN = [120, 128]
    C_LO, C_HI, C_LOH, C_HIH = 184, 185, 186, 187
    C_XST = 176
    C_M64 = 192
    C_TMP = 256

    PHYS = [ps(f"bk{i}", [128, 1024], BF16) for i in range(8)]
    bind = {}
    vb_kind = []

    def new_vb(kind):
        vb_kind.append(kind)
        return len(vb_kind) - 1

    class _Lazy:
        def __init__(self, f):
            self.f = f

        def __getitem__(self, v):
            return self.f(v)

    PA = _Lazy(lambda v: PHYS[bind.get(v, 0)][:].bitcast(F32))
    PSB = PA
    PTRS = _Lazy(lambda v: PHYS[bind.get(v, 0)])
    VR = lambda v: [("vb", v, 0), ("vb", v, 1)]

    engsems = {e: sem("s_" + e) for e in ("pe", "act", "dve", "pool")}
    dsem_names = ["w0", "w1", "w2", "w3", "w4", "w5", "w6", "w7", "w8", "c0", "c1", "x0", "x1", "y0", "y1", "kv", "cache", "cpy", "vv", "dn",
                  "cv0", "cv1", "cv2", "cv3", "cv4", "xt", "ogt", "kv0", "kv1", "cva0", "cva1"]
    dsems = {n: sem("d_" + n) for n in dsem_names}

    GROUP_SEMS = [dsems[n] for n in ("w0", "w1", "w2", "w3", "w4", "w5", "w6", "w7", "w8", "c0", "c1", "vv", "dn", "cpy")]

    def dma(engname, out, in_, reads, writes, dsem, transpose=False, **kw):
        if transpose:
            S.add(engname, lambda e, out=out, in_=in_: e.dma_start_transpose(out=out, in_=in_),
                  reads=reads, writes=writes, dsem=dsems[dsem])
        else:
            S.add(engname, lambda e, out=out, in_=in_, kw=kw: e.dma_start(out=out, in_=in_, **kw),
                  reads=reads, writes=writes, dsem=dsems[dsem])

    dma("pool", IDENT[:], ident_d[:, :], [], ["ident"], "c0")
    dma("pool", ADM[:], adm_d[:, :], [], ["adm"], "c0")
    dma("sp", T5S, t5[:, :], [], ["t5s"], "c1")
    dma("sp", RBS, rbT.rearrange("k p h -> p k h"), [], ["rbs"], "c1")
    dma("sp", OHAS, oha_d[:, :], [], ["ohas"], "c1")
    dma("sp", OHBS, ohb_d.rearrange("k p n -> p k n"), [], ["ohbs"], "c1")
    dma("sp", SM[:, C_MASK:C_MASK + 1], maskv_d[:, :], [], ["c_mask"], "c1")

    def bcast_src(ap1d, n):
        return bass.AP(tensor=ap1d.tensor, offset=ap1d.offset, ap=[[0, 128], [1, n]])

    def col_src(ap1d, n, off=0):
        return bass.AP(tensor=ap1d.tensor, offset=ap1d.offset + off, ap=[[1, n], [1, 1]])

    for l, (qn_, kn_) in enumerate(((qn_a, kn_a), (qn_b, kn_b))):
        dma("sp", GKBC[l][:], bcast_src(kn_, 64), [], [f"gkbc{l}"], "c1")
        dma("sp", GQBC[l], bcast_src(qn_, 64), [], [f"gqbc{l}"], "c1")
    for l, nrm in enumerate((norm_a, norm_b)):
        for k in range(8):
            dma("sp", SM[:, C_GN[l] + k:C_GN[l] + k + 1], col_src(nrm, 128, k * 128), [], [(f"c_gn{l}", k)], "c1")
    dma("sp", ESBC[:], bcast_src(sinks, 16), [], ["esbc"], "c1")

    S.add("dve", lambda e: e.memset(SM[:, C_MHALF:C_MHALF + 32], -0.5), writes=["c_mhalf"])
    S.add("dve", lambda e: e.memset(SM[:, C_ZERO:C_ZERO + 1], 0.0), writes=["c_zero"])
    S.add("dve", lambda e: e.memset(SM[:, C_LO:C_HI + 1], 0.0), writes=["c_lohi"])
    S.add("dve", lambda e: e.memset(SM[0:64, C_LO:C_LO + 1], MASKV), reads=["c_lohi"], writes=["c_lohi"])
    S.add("dve", lambda e: e.memset(SM[64:128, C_HI:C_HI + 1], MASKV), reads=["c_lohi"], writes=["c_lohi"])
    S.add("dve", lambda e: e.tensor_scalar(out=SM[:, C_LOH:C_HIH + 1], in0=SM[:, C_LO:C_HI + 1], scalar1=SM[:, C_MASK:C_MASK + 1],
                                           scalar2=None, op0=ALU.add), reads=["c_lohi", "c_mask"], writes=["c_mask"])
    S.add("dve", lambda e: e.tensor_tensor(out=SM[:, C_M64:C_M64 + 64], in0=IDENT[:, 0:64], in1=IDENT[:, 64:128], op=ALU.add),
          reads=["ident"], writes=["c_m64"])
    for l in range(2):
        S.add("dve", lambda e, l=l: e.tensor_tensor(out=SM[:, C_TMP:C_TMP + 64], in0=GQBC[l], in1=GKBC[l][:], op=ALU.mult),
              reads=[f"gqbc{l}", f"gkbc{l}", ("x", 1)], writes=["c_tmp"])
        S.add("dve", lambda e, l=l: e.tensor_tensor(out=SM[:, C_TMP:C_TMP + 64], in0=SM[:, C_TMP:C_TMP + 64], in1=SM[:, C_M64:C_M64 + 64],
                                                    op=ALU.mult), reads=["c_tmp", "c_m64"], writes=["c_tmp"])
        S.add("dve", lambda e, l=l: e.tensor_reduce(out=SM[:, C_GQK[l]:C_GQK[l] + 1], in_=SM[:, C_TMP:C_TMP + 64], axis=AX.X, op=ALU.add),
              reads=["c_tmp"], writes=[f"c_gqk{l}"])
        S.add("dve", lambda e, l=l: e.tensor_scalar(out=SM[:, C_GQK[l]:C_GQK[l] + 1], in0=SM[:, C_GQK[l]:C_GQK[l] + 1],
                                                    scalar1=0.125, scalar2=None, op0=ALU.mult),
              reads=[f"c_gqk{l}"], writes=[f"c_gqk{l}"])
    S.add("act", lambda e: e.activation(out=ESBC[:], in_=ESBC[:], func=AF.Exp), reads=["esbc"], writes=["esbc"])
    S.add("dve", lambda e: e.tensor_scalar(out=ESBC[:], in0=ESBC[:], scalar1=ONESV, scalar2=None, op0=ALU.mult),
          reads=["esbc"], writes=["esbc"])
    def qt_res(qs_):
        return [("qT", qs_, h_, p_) for h_ in range(2) for p_ in range(2)]

    for i in range(NQS):
        S.add("pool", lambda e, i=i: e.memset(QTS[i][:], 0.0), writes=qt_res(i))
    for i in range(NKB):
        S.add("pool", lambda e, i=i: e.memset(VB[i][:, :, 64:65], ONESV), writes=[("VB", i)])
    for i in range(3):
        S.add("pool", lambda e, i=i: e.memset(VA[i][:, :, 64:65], ONESV), writes=[("VA", i)])


    vvb = new_vb("score")

    def vv_mm(e):
        e.matmul(PSB[vvb][0:16, 0:256], lhsT=T5S, rhs=OHAS, start=True, stop=True)
        ins = None
        for k in range(3):
            ins = e.matmul(PSB[vvb][0:16, 256:512], lhsT=RBS[:, k, :], rhs=OHBS[:, k, :], start=(k == 0), stop=(k == 2),
                           skip_group_check=True)
        return ins
    S.add("pe", vv_mm, reads=["t5s", "rbs", "ohas", "ohbs", "sq", ("x", 1)], writes=VR(vvb))
    S.add("dve", lambda e: e.tensor_copy(out=XS[1][0:16, 0:512], in_=PSB[vvb][0:16, :]),
          reads=VR(vvb), writes=["vvs"])
    dma("sp", vv_d.rearrange("l h n -> h l n"), VVS, ["vvs", ("x", 1)], ["vv_d"], "vv")
    for l in range(2):
        src1 = bass.AP(tensor=vv_d.tensor, offset=vv_d.offset + l * 16 * 256, ap=[[1, 128], [256, 16], [1, 64]])
        src2 = bass.AP(tensor=vv_d.tensor, offset=vv_d.offset + l * 16 * 256 + 128, ap=[[1, 64], [256, 16], [1, 64]])
        dma("pool", DN1[l][:], src1, ["vv_d"], [f"dn1_{l}"], "dn")
        dma("pool", DN2M[l * 64:(l + 1) * 64, :, :], src2, ["vv_d"], [f"dn2_{l}"], "dn")


    def load_w(Wt, wd, c0, ncols, dname, res, l_gain, after=()):
        for k in range(8):
            dma("pool", Wt[:, k, c0:c0 + ncols], wd[k * 128:(k + 1) * 128, c0:c0 + ncols], list(after), [(res, c0, k)], dname)
        if l_gain is not None:
            for k in range(8):
                c = C_GN[l_gain] + k
                S.add("dve", lambda e, k=k, c=c: e.tensor_scalar(out=Wt[:, k, c0:c0 + ncols], in0=Wt[:, k, c0:c0 + ncols],
                                                                 scalar1=SM[:, c:c + 1], scalar2=None, op0=ALU.mult),
                      reads=[(res, c0, k), (f"c_gn{l_gain}", k)], writes=[(res, c0, k)])
    load_w(WA, w_in_a, 1024, 256, "w0", "WA", 0)
    load_w(WA, w_in_a, 0, 1024, "w1", "WA", 0)
    load_w(WA, w_in_a, 1280, 1024, "w2", "WA", 0)

    def late_weights(stage, after=()):
        S.cur = []
        if stage == 0:
            load_w(WOA, w_out_a, 0, 1024, "w3", "WOA", None, after)
            load_w(WB, w_in_b, 1024, 1024, "w4", "WB", 1, after)
            load_w(WB, w_in_b, 2048, 1024, "w5", "WB", 1, after)
        else:
            load_w(WB, w_in_b, 0, 1024, "w6", "WB", 1, after)
            load_w(WB, w_in_b, 3072, 1024, "w7", "WB", 1, after)
            load_w(WOB, w_out_b, 0, 1024, "w8", "WOB", None, after)
        ops = S.cur
        S.cur = []
        return ops

    pa_ctr = [0]
    psb_ctr = [0]

    def next_ptr():
        return new_vb("tr")

    def next_pa():
        return new_vb("proj")

    def next_psb():
        return new_vb("score")

    def layer_cfg(l):
        if l == 0:
            return dict(W=WA, Wres="WA", WO=WOA, WOres="WOA", qoff=0, koff=1024, voff=1152, goff=1280, nkv=2)
        return dict(W=WB, Wres="WB", WO=WOB, WOres="WOB", qoff=0, koff=1024, voff=2048, goff=3072, nkv=16)

    def norm_and_transpose(xs, l):
        X = XS[xs]
        xr = ("x", xs)
        S.add("act", lambda e: e.activation(out=SQ[:], in_=X[:], func=AF.Square, scale=1.0 / 32.0,
                                            accum_out=SM[:, C_SS:C_SS + 1]),
              reads=[xr], writes=["sq", "c_ss"])
        S.add("dve", lambda e: e.tensor_scalar(out=SM[:, C_SS:C_SS + 1], in0=SM[:, C_SS:C_SS + 1], scalar1=EPS, scalar2=None,
                                               op0=ALU.add), reads=["c_ss"], writes=["c_ss"])
        S.add("pool", lambda e: e.tensor_tensor(out=SM[:, C_RX:C_RX + 1], in0=SM[:, C_SS:C_SS + 1],
                                                in1=SM[:, C_MHALF:C_MHALF + 1], op=ALU.pow),
              reads=["c_ss", "c_mhalf"], writes=["c_rx"])
        S.add("dve", lambda e: e.tensor_scalar(out=XN[:], in0=X[:], scalar1=SM[:, C_RX:C_RX + 1], scalar2=None, op0=ALU.mult),
              reads=[xr, "c_rx"], writes=["xn"])

        if DMA_T:
            for k in range(8):
                dma("sp", XT[:, k, :], XN[:, k * 128:(k + 1) * 128], ["xn"], [("xT", k)], "xt", transpose=True)
        else:
            def tr(e):
                ins = None
                for k in range(8):
                    ins = e.transpose(out=PTR[:, k * 128:(k + 1) * 128], in_=XN[:, k * 128:(k + 1) * 128], identity=IDENT[:])
                return ins
            S.add("pe", tr, reads=["xn", "ident"], writes=["ptr"])
            S.add("act", lambda e: e.activation(out=XT[:].rearrange("p k t -> p (k t)"), in_=PTR[:], func=AF.Copy),
                  reads=["ptr"], writes=[("xT", k) for k in range(8)])

    def proj_group(W, Wres, c0, ncols, pa):
        def mm(e):
            ins = None
            for k in range(8):
                for j in range(0, ncols, 512):
                    w = min(512, ncols - j)
                    ins = e.matmul(PA[pa][:, j:j + w], lhsT=XT[:, k, :], rhs=W[:, k, c0 + j:c0 + j + w],
                                   start=(k == 0), stop=(k == 7), skip_group_check=True)
            return ins
        S.add("pe", mm, reads=[("xT", k) for k in range(8)] + [(Wres, c0, k) for k in range(8)], writes=[("pa", pa)])

    def head_rstd(pa, c0, nh, l):
        w = nh * 64
        S.add("act", lambda e: e.activation(out=SQ[:, 0:w], in_=PA[pa][:, c0:c0 + w], func=AF.Square, scale=0.125),
              reads=[("pa", pa)], writes=["sq"])
        S.add("dve", lambda e: e.tensor_reduce(out=SM[:, C_SSQ:C_SSQ + nh], in_=SQ[:, 0:w].rearrange("p (h d) -> p h d", d=64),
                                               axis=AX.X, op=ALU.add), reads=["sq"], writes=["c_ssq"])
        S.add("dve", lambda e: e.tensor_scalar(out=SM[:, C_SSQ:C_SSQ + nh], in0=SM[:, C_SSQ:C_SSQ + nh], scalar1=EPS, scalar2=None,
                                               op0=ALU.add), reads=["c_ssq"], writes=["c_ssq"])
        S.add("pool", lambda e: e.tensor_tensor(out=SM[:, C_RQ:C_RQ + nh], in0=SM[:, C_SSQ:C_SSQ + nh],
                                                in1=SM[:, C_MHALF:C_MHALF + nh], op=ALU.pow),
              reads=["c_ssq", "c_mhalf"], writes=["c_rq"])

    def bc3(ap2d, nh, n):
        return bass.AP(tensor=ap2d.tensor, offset=ap2d.offset, ap=[list(ap2d.ap[0]), [1, nh], [0, n]])

    def kv_out_rows(kdst_fn, pa, c0, w, nh, l):
        S.add("dve", lambda e: e.tensor_tensor(out=SQ[:, 0:w].rearrange("p (h d) -> p h d", d=64),
                                               in0=PA[pa][:, c0:c0 + w].rearrange("p (h d) -> p h d", d=64),
                                               in1=bc3(SM[:, C_RQ:C_RQ + nh], nh, 64), op=ALU.mult),
              reads=[("pa", pa), "c_rq"], writes=["sq"])
        gk = GKBC[l]
        gkb = bass.AP(tensor=gk[:].tensor, offset=gk[:].offset, ap=[list(gk[:].ap[0]), [0, nh], [1, 64]])
        S.add("dve", lambda e: e.tensor_tensor(out=SQ[:, 0:w].rearrange("p (h d) -> p h d", d=64),
                                               in0=SQ[:, 0:w].rearrange("p (h d) -> p h d", d=64), in1=gkb, op=ALU.mult),
              reads=["sq", f"gkbc{l}"], writes=["sq"])
        kdst_fn(w)

    def v_out_rows(vdst_fn, pa, c0, w):
        S.add("act", lambda e: e.activation(out=SQ[:, 0:w], in_=PA[pa][:, c0:c0 + w], func=AF.Copy),
              reads=[("pa", pa)], writes=["sq"])
        vdst_fn(w)

    def sq_store(dsts):
        def f(w):
            for (d, p0, p1) in dsts:
                dma("sp", d, SQ[p0:p1, 0:w], ["sq"], [], "kv")
        return f

    def in_proj(l, xs, aslot, bslot, kdst=None, vdst=None, need_q=True, qs=0):
        QT, GA = QTS[qs], GAS[qs]
        cfg = layer_cfg(l)
        W, Wres = cfg["W"], cfg["Wres"]
        outer = S.cur
        pieces = {}

        def piece(name):
            S.cur = pieces.setdefault(name, [])
        piece("N")
        norm_and_transpose(xs, l)
        gqk = SM[:, C_GQK[l]:C_GQK[l] + 1]
        if need_q:
            pa = next_pa()
            piece("Pq")
            proj_group(W, Wres, 0, 1024, pa)
            piece("Cq")
            head_rstd(pa, 0, 16, l)
            S.add("dve", lambda e, pa=pa: e.tensor_tensor(out=QN[:].rearrange("p (h d) -> p h d", d=64),
                                                   in0=PA[pa][:, 0:1024].rearrange("p (h d) -> p h d", d=64),
                                                   in1=bc3(SM[:, C_RQ:C_RQ + 16], 16, 64), op=ALU.mult),
                  reads=[("pa", pa), "c_rq"], writes=["qn"])

            piece("Tq")

            def trq(e):
                ins = None
                for k in range(8):
                    ins = e.transpose(out=PTR[:, k * 128:(k + 1) * 128], in_=QN[:, k * 128:(k + 1) * 128], identity=IDENT[:])
                return ins
            S.add("pe", trq, reads=["qn", "ident"], writes=["ptr"])
            S.add("act", lambda e, pa=pa: e.mul(out=QT[0:64, :, 0, :], in_=PTR[0:64, :].rearrange("p (k t) -> p k t", t=128),
                                         mul=gqk[0:64, :]),
                  reads=["ptr", f"c_gqk{l}"], writes=qt_res(qs))
            S.add("dve", lambda e, pa=pa: e.tensor_scalar(out=QT[64:128, :, 1, :], in0=PTR[64:128, :].rearrange("p (k t) -> p k t", t=128),
                                                   scalar1=gqk[64:128, :], scalar2=None, op0=ALU.mult),
                  reads=["ptr", f"c_gqk{l}"], writes=qt_res(qs))
        if l == 0:
            pa = next_pa()
            piece("Pk")
            proj_group(W, Wres, 1024, 256, pa)
            piece("Ck")
            head_rstd(pa, 0, 2, l)
            kin = PA[pa][:, 0:128]
            kin_b = bass.AP(tensor=kin.tensor, offset=kin.offset, ap=[list(kin.ap[0]), [64, 2], [0, 2], [1, 64]])
            rq = SM[:, C_RQ:C_RQ + 2]
            rq_b = bass.AP(tensor=rq.tensor, offset=rq.offset, ap=[list(rq.ap[0]), [1, 2], [0, 2], [0, 64]])
            S.add("dve", lambda e, pa=pa: e.tensor_tensor(out=QN[:, 0:256].rearrange("p (g u d) -> p g u d", g=2, u=2), in0=kin_b, in1=rq_b,
                                                   op=ALU.mult), reads=[("pa", pa), "c_rq"], writes=["qn"])

            piece("Tk")

            def trk(e):
                ins = None
                for g in range(2):
                    ins = e.transpose(out=PTR[:, g * 128:(g + 1) * 128], in_=QN[:, g * 128:(g + 1) * 128], identity=IDENT[:])
                return ins
            S.add("pe", trk, reads=["qn", "ident"], writes=["ptr"])
            S.add("act", lambda e, pa=pa: e.activation(out=KTA[aslot][:].rearrange("p g t -> p (g t)"), in_=PTR[:, 0:256], func=AF.Copy),
                  reads=["ptr"], writes=[("KTA", aslot)])
            S.add("dve", lambda e, pa=pa: e.tensor_copy(out=VA[aslot][:, :, 0:64], in_=PA[pa][:, 128:256].rearrange("p (g d) -> p g d", d=64)),
                  reads=[("pa", pa)], writes=[("VA", aslot)])
            if kdst is not None:
                kv_out_rows(kdst, pa, 0, 128, 2, l)
                v_out_rows(vdst, pa, 128, 128)
        else:
            pa = next_pa()
            piece("Pk")
            proj_group(W, Wres, 1024, 1024, pa)
            piece("Ck")
            head_rstd(pa, 0, 16, l)
            S.add("dve", lambda e, pa=pa: e.tensor_tensor(out=QN[:].rearrange("p (h d) -> p h d", d=64),
                                                   in0=PA[pa][:, 0:1024].rearrange("p (h d) -> p h d", d=64),
                                                   in1=bc3(SM[:, C_RQ:C_RQ + 16], 16, 64), op=ALU.mult),
                  reads=[("pa", pa), "c_rq"], writes=["qn"])

            piece("Tk")

            def trk(e):
                ins = None
                for k in range(8):
                    ins = e.transpose(out=PTR[:, k * 128:(k + 1) * 128], in_=QN[:, k * 128:(k + 1) * 128], identity=IDENT[:])
                return ins
            S.add("pe", trk, reads=["qn", "ident"], writes=["ptr"])
            S.add("act", lambda e, pa=pa: e.activation(out=KTB[bslot][:].rearrange("p k t -> p (k t)"), in_=PTR[:], func=AF.Copy),
                  reads=["ptr"], writes=[("KTB", bslot)])
            if kdst is not None:
                kv_out_rows(kdst, pa, 0, 1024, 16, l)
            pa2 = next_pa()
            piece("Pv")
            proj_group(W, Wres, 2048, 1024, pa2)
            piece("Cv")
            S.add("dve", lambda e, pa2=pa2: e.tensor_copy(out=VB[bslot][:, :, 0:64], in_=PA[pa2][:, 0:1024].rearrange("p (h d) -> p h d", d=64)),
                  reads=[("pa", pa2)], writes=[("VB", bslot)])
            if vdst is not None:
                v_out_rows(vdst, pa2, 0, 1024)
        if need_q:
            pa = next_pa()
            piece("Pg")
            proj_group(W, Wres, cfg["goff"], 1024, pa)
            piece("Cg")
            S.add("act", lambda e, pa=pa: e.activation(out=GA[:], in_=PA[pa][:, 0:1024], func=AF.Tanh, scale=0.5),
                  reads=[("pa", pa)], writes=[("ga", qs, 0), ("ga", qs, 1)])
            S.add("dve", lambda e, pa=pa: e.scalar_tensor_tensor(out=GA[:], in0=GA[:], scalar=1.0, in1=PA[pa][:, 0:1024],
                                                          op0=ALU.add, op1=ALU.mult),
                  reads=[("ga", qs, 0), ("ga", qs, 1), ("pa", pa)], writes=[("ga", qs, 0), ("ga", qs, 1)])
        S.cur = outer
        for name in INPROJ_ORDER:
            S.cur.extend(pieces.get(name, []))
        assert set(pieces) <= set(INPROJ_ORDER), pieces.keys()

    PAR = VR
    PTR2 = VR
    PTR1 = lambda i, p: ("vb", i, p)

    xstat_ctr = [0]
    xbuf_ctr = [0]

    def sq_store2(dsts):
        def f(sq_off, w, dcol):
            for (d, p0, p1) in dsts:
                dma("sp", d[:, dcol:dcol + w], SQ[p0:p1, sq_off:sq_off + w], [("sq", sq_off // 512)], [], f"kv{sq_off // 512}")
        return f

    def in_proj2(l, xs, aslot, bslot, kdst=None, vdst=None, need_q=True, qs=0):
        QT, GA = QTS[qs], GAS[qs]
        cfg = layer_cfg(l)
        W, Wres = cfg["W"], cfg["Wres"]
        X = XS[xs]
        xr = ("x", xs)
        xb = xbuf_ctr[0] % NXB
        xbuf_ctr[0] += 1
        XN, XT = XNS[xb], XTS[xb]
        par = xstat_ctr[0] % 2
        xstat_ctr[0] += 1
        cSS, cRX, cE2, cRXH = C_XST + 4 * par, C_XST + 4 * par + 1, C_XST + 4 * par + 2, C_XST + 4 * par + 3
        rxs = ("c_rx", par)
        S.add("act", lambda e: e.activation(out=XN[:], in_=X[:], func=AF.Square, scale=1.0 / 32.0, accum_out=SM[:, cSS:cSS + 1]),
              reads=[xr], writes=[("xn", xb), ("c_ss", par)])
        S.add("dve", lambda e: e.tensor_copy(out=XN[:], in_=X[:]), reads=[xr], writes=[("xn", xb)])
        S.add("dve", lambda e: e.tensor_scalar(out=SM[:, cSS:cSS + 1], in0=SM[:, cSS:cSS + 1], scalar1=EPS, scalar2=None,
                                               op0=ALU.add), reads=[("c_ss", par)], writes=[("c_ss", par)])
        S.add("dve", lambda e: e.tensor_scalar(out=SM[:, cE2:cE2 + 1], in0=SM[:, cSS:cSS + 1], scalar1=EPS, scalar2=None,
                                               op0=ALU.mult), reads=[("c_ss", par)], writes=[("c_e2", par)])
        S.add("pool", lambda e: e.tensor_tensor(out=SM[:, cRX:cRX + 1], in0=SM[:, cSS:cSS + 1],
                                                in1=SM[:, C_MHALF:C_MHALF + 1], op=ALU.pow),
              reads=[("c_ss", par), "c_mhalf"], writes=[rxs])
        S.add("dve", lambda e: e.tensor_scalar(out=SM[:, cRXH:cRXH + 1], in0=SM[:, cRX:cRX + 1], scalar1=0.5, scalar2=None,
                                               op0=ALU.mult), reads=[rxs], writes=[("c_rxh", par)])
        RX = SM[:, cRX:cRX + 1]
        RXH = SM[:, cRXH:cRXH + 1]
        E2 = SM[:, cE2:cE2 + 1]
        pt = next_ptr()

        def trx(e, pt=pt):
            ins = None
            for k in range(8):
                ins = e.transpose(out=PTRS[pt][:, k * 128:(k + 1) * 128], in_=XN[:, k * 128:(k + 1) * 128], identity=IDENT[:])
            return ins
        S.add("pe", trx, reads=[("xn", xb), "ident"], writes=[*PTR2(pt)])
        S.add("act", lambda e, pt=pt: e.activation(out=XT[:].rearrange("p k t -> p (k t)"), in_=PTRS[pt][:], func=AF.Copy),
              reads=[*PTR2(pt)], writes=[("xT", xb, k) for k in range(8)])
        gqk = SM[:, C_GQK[l]:C_GQK[l] + 1]

        def proj(c0, ncols, wkey):
            pa = next_pa()

            def mm(e):
                ins = None
                for k in range(8):
                    ins = e.matmul(PA[pa][:, 0:ncols], lhsT=XT[:, k, :], rhs=W[:, k, c0:c0 + ncols],
                                   start=(k == 0), stop=(k == 7), skip_group_check=True)
                return ins
            S.add("pe", mm, reads=[("xT", xb, k) for k in range(8)] + [(Wres, wkey, k) for k in range(8)], writes=[*PAR(pa)])
            return pa

        def rstd_chain(pa, pcol, nh, sq_off, ccol, cres):
            w = nh * 64
            hs = sq_off // 512
            S.add("act", lambda e: e.activation(out=SQ[:, sq_off:sq_off + w], in_=PA[pa][:, pcol:pcol + w], func=AF.Square, scale=0.125),
                  reads=[*PAR(pa)], writes=[("sq", hs)])
            S.add("dve", lambda e: e.tensor_reduce(out=SM[:, C_SSQ + ccol:C_SSQ + ccol + nh],
                                                   in_=SQ[:, sq_off:sq_off + w].rearrange("p (h d) -> p h d", d=64),
                                                   axis=AX.X, op=ALU.add), reads=[("sq", hs)], writes=[("c_ssq", cres)])
            S.add("dve", lambda e: e.tensor_scalar(out=SM[:, C_SSQ + ccol:C_SSQ + ccol + nh], in0=SM[:, C_SSQ + ccol:C_SSQ + ccol + nh],
                                                   scalar1=E2, scalar2=None, op0=ALU.add), reads=[("c_ssq", cres), ("c_e2", par)], writes=[("c_ssq", cres)])
            S.add("pool", lambda e: e.tensor_tensor(out=SM[:, C_RQ + ccol:C_RQ + ccol + nh], in0=SM[:, C_SSQ + ccol:C_SSQ + ccol + nh],
                                                    in1=SM[:, C_MHALF:C_MHALF + nh], op=ALU.pow),
                  reads=[("c_ssq", cres), "c_mhalf"], writes=[("c_rq", cres)])

        def k_out(pa, pcol, nh, sq_off, ccol, cres, dcol):
            w = nh * 64
            hs = sq_off // 512
            sqv = SQ[:, sq_off:sq_off + w].rearrange("p (h d) -> p h d", d=64)
            S.add("dve", lambda e: e.tensor_tensor(out=sqv, in0=PA[pa][:, pcol:pcol + w].rearrange("p (h d) -> p h d", d=64),
                                                   in1=bc3(SM[:, C_RQ + ccol:C_RQ + ccol + nh], nh, 64), op=ALU.mult),
                  reads=[*PAR(pa), ("c_rq", cres)], writes=[("sq", hs)])
            gk = GKBC[l]
            gkb = bass.AP(tensor=gk[:].tensor, offset=gk[:].offset, ap=[list(gk[:].ap[0]), [0, nh], [1, 64]])
            S.add("dve", lambda e: e.tensor_tensor(out=sqv, in0=sqv, in1=gkb, op=ALU.mult),
                  reads=[("sq", hs), f"gkbc{l}"], writes=[("sq", hs)])
            kdst(sq_off, w, dcol)

        def v_out(pa, pcol, w, sq_off, dcol):
            hs = sq_off // 512
            S.add("act", lambda e: e.mul(out=SQ[:, sq_off:sq_off + w], in_=PA[pa][:, pcol:pcol + w], mul=RX),
                  reads=[*PAR(pa), rxs], writes=[("sq", hs)])
            vdst(sq_off, w, dcol)

        def norm_T(pa, h, cres, dst_kind):
            ccol = 8 * cres
            S.add("dve", lambda e: e.tensor_tensor(out=QN[:, h * 512:(h + 1) * 512].rearrange("p (h d) -> p h d", d=64),
                                                   in0=PA[pa][:, 0:512].rearrange("p (h d) -> p h d", d=64),
                                                   in1=bc3(SM[:, C_RQ + ccol:C_RQ + ccol + 8], 8, 64), op=ALU.mult),
                  reads=[*PAR(pa), ("c_rq", cres)], writes=[("qn", h)])
            pt = next_ptr()

            def tr(e):
                ins = None
                for j in range(4):
                    ins = e.transpose(out=PTRS[pt][:, j * 128:(j + 1) * 128], in_=QN[:, h * 512 + j * 128:h * 512 + (j + 1) * 128],
                                      identity=IDENT[:])
                return ins
            S.add("pe", tr, reads=[("qn", h), "ident"], writes=[*PTR2(pt)])
            if dst_kind == "q":
                S.add("act", lambda e: e.mul(out=QT[0:64, 4 * h:4 * h + 4, 0, :],
                                             in_=PTRS[pt][0:64, 0:512].rearrange("p (k t) -> p k t", t=128), mul=gqk[0:64, :]),
                      reads=[PTR1(pt, 0), f"c_gqk{l}"], writes=[("qT", qs, h, 0)])
                S.add("dve", lambda e: e.tensor_scalar(out=QT[64:128, 4 * h:4 * h + 4, 1, :],
                                                       in0=PTRS[pt][64:128, 0:512].rearrange("p (k t) -> p k t", t=128),
                                                       scalar1=gqk[64:128, :], scalar2=None, op0=ALU.mult),
                      reads=[PTR1(pt, 1), f"c_gqk{l}"], writes=[("qT", qs, h, 1)])
            else:
                S.add("act", lambda e: e.activation(out=KTB[bslot][:, 4 * h:4 * h + 4, :].rearrange("p k t -> p (k t)"),
                                                    in_=PTRS[pt][:, 0:512], func=AF.Copy),
                      reads=[*PTR2(pt)], writes=[("KTB", bslot)])

        if need_q:
            for h in range(2):
                pa = proj(h * 512, 512, 0)
                rstd_chain(pa, 0, 8, h * 512, 8 * h, h)
                norm_T(pa, h, h, "q")
        if l == 0:
            pa = proj(1024, 256, 1024)
            rstd_chain(pa, 0, 2, 0, 16, 2)
            def knd(e, pa=pa):
                kin = PA[pa][:, 0:128]
                kin_b = bass.AP(tensor=kin.tensor, offset=kin.offset, ap=[list(kin.ap[0]), [64, 2], [0, 2], [1, 64]])
                rq = SM[:, C_RQ + 16:C_RQ + 18]
                rq_b = bass.AP(tensor=rq.tensor, offset=rq.offset, ap=[list(rq.ap[0]), [1, 2], [0, 2], [0, 64]])
                return e.tensor_tensor(out=QN[:, 0:256].rearrange("p (g u d) -> p g u d", g=2, u=2), in0=kin_b, in1=rq_b, op=ALU.mult)
            S.add("dve", knd, reads=[*PAR(pa), ("c_rq", 2)], writes=[("qn", 0)])
            pt = next_ptr()

            def trk(e):
                ins = None
                for g in range(2):
                    ins = e.transpose(out=PTRS[pt][:, g * 128:(g + 1) * 128], in_=QN[:, g * 128:(g + 1) * 128], identity=IDENT[:])
                return ins
            S.add("pe", trk, reads=[("qn", 0), "ident"], writes=[*PTR2(pt)])
            S.add("act", lambda e: e.activation(out=KTA[aslot][:].rearrange("p g t -> p (g t)"), in_=PTRS[pt][:, 0:256], func=AF.Copy),
                  reads=[*PTR2(pt)], writes=[("KTA", aslot)])
            S.add("dve", lambda e, pa=pa: e.tensor_scalar(out=VA[aslot][:, :, 0:64], in0=PA[pa][:, 128:256].rearrange("p (g d) -> p g d", d=64),
                                                              scalar1=RX, scalar2=None, op0=ALU.mult),
                  reads=[*PAR(pa), rxs], writes=[("VA", aslot)])
            if kdst is not None:
                k_out(pa, 0, 2, 0, 16, 2, 0)
                v_out(pa, 128, 128, 512, 0)
        else:
            for h in range(2):
                pa = proj(1024 + h * 512, 512, 1024)
                rstd_chain(pa, 0, 8, h * 512, 16 + 8 * h, 2 + h)
                norm_T(pa, h, 2 + h, "k")
                if kdst is not None:
                    k_out(pa, 0, 8, h * 512, 16 + 8 * h, 2 + h, h * 512)
            for h in range(2):
                pa = proj(2048 + h * 512, 512, 2048)
                S.add("dve", lambda e, pa=pa, h=h: e.tensor_scalar(out=VB[bslot][:, 8 * h:8 * h + 8, 0:64],
                                                                   in0=PA[pa][:, 0:512].rearrange("p (h d) -> p h d", d=64),
                                                                   scalar1=RX, scalar2=None, op0=ALU.mult),
                      reads=[*PAR(pa), rxs], writes=[("VB", bslot)])
                if vdst is not None:
                    v_out(pa, 0, 512, h * 512, h * 512)
        if need_q:
            for h in range(2):
                pa = proj(cfg["goff"] + h * 512, 512, cfg["goff"])
                S.add("act", lambda e, pa=pa, h=h: e.activation(out=GA[:, h * 512:(h + 1) * 512], in_=PA[pa][:, 0:512], func=AF.Tanh, scale=RXH),
                      reads=[*PAR(pa), ("c_rxh", par)], writes=[("ga", qs, h)])
                S.add("dve", lambda e, pa=pa, h=h: e.scalar_tensor_tensor(out=GA[:, h * 512:(h + 1) * 512], in0=GA[:, h * 512:(h + 1) * 512],
                                                                          scalar=1.0, in1=PA[pa][:, 0:512], op0=ALU.add, op1=ALU.mult),
                      reads=[("ga", qs, h), *PAR(pa)], writes=[("ga", qs, h)])
                S.add("dve", lambda e, h=h: e.tensor_scalar(out=GA[:, h * 512:(h + 1) * 512], in0=GA[:, h * 512:(h + 1) * 512],
                                                            scalar1=RX, scalar2=None, op0=ALU.mult),
                      reads=[("ga", qs, h), rxs], writes=[("ga", qs, h)])

    BIAS_PREV = [(191, 128, 1, 0), (63, 64, 2, 0), (127, 64, 2, 1)]
    BIAS_CUR = [(63, 128, 1, 0), (127, 128, 1, 1)]

    def BIAS_CACHE(s):
        return [(191, 128, 1, s), (63, 64, 2, s)]

    def attention(l, key_tiles, rows, qs=0):
        QT = QTS[qs]
        r0, r1 = rows
        nkt = len(key_tiles)

        def mk_qk(hg, ti, kt, b):
            L2 = len(kt["bias"]) > 0

            def qk(e):
                ins = None
                nb = len(kt["bias"])
                first = True
                for pl in range(2):
                    pair = hg * 2 + pl
                    if not L2:
                        ins = e.matmul(PSB[b][:, pl * 256:(pl + 1) * 256], lhsT=kt["kt"](pair),
                                       rhs=QT[:, pair, :, :].rearrange("p a t -> p (a t)"),
                                       start=first, stop=(pl == 1), skip_group_check=True)
                        first = False
                    else:
                        for par in range(2):
                            ins = e.matmul(PSB[b][:, par * 256 + pl * 128:par * 256 + (pl + 1) * 128], lhsT=kt["kt"](pair),
                                           rhs=QT[:, pair, :, par * 64:(par + 1) * 64],
                                           start=first, stop=False, skip_group_check=True)
                            first = False
                for bi, (c, K, dn, par) in enumerate(kt["bias"]):
                    if dn == 1:
                        lt, rh = ADM[0:K, 255 - c:255 - c + 128], DN1[l][0:K, hg * 4:(hg + 1) * 4, :]
                    else:
                        lt = ADM[l * 64:l * 64 + 64, 255 - c - 64 * l:255 - c - 64 * l + 128]
                        rh = DN2M[l * 64:l * 64 + 64, hg * 4:(hg + 1) * 4, :]
                    ins = e.matmul(PSB[b][:, par * 256:(par + 1) * 256], lhsT=lt, rhs=rh.rearrange("p h t -> p (h t)"),
                                   start=False, stop=(bi == nb - 1), skip_group_check=True)
                return ins
            return qk

        def add_exp(ti, kt, b):
            L2 = len(kt["bias"]) > 0
            em = kt["em"]
            if em[0] == em[1]:
                if L2:
                    o_ap = PT[:, ti, :].rearrange("p (h par t) -> p par h t", h=4, par=2)
                    i_ap = PSB[b][:, :].rearrange("p (par h t) -> p par h t", par=2, h=4)
                else:
                    o_ap, i_ap = PT[:, ti, :], PSB[b][:, :]
                def ex(e):
                    if L2:
                        o_ap = PT[:, ti, :].rearrange("p (h par t) -> p par h t", h=4, par=2)
                        i_ap = PSB[b][:, :].rearrange("p (par h t) -> p par h t", par=2, h=4)
                    else:
                        o_ap, i_ap = PT[:, ti, :], PSB[b][:, :]
                    return e.activation(out=o_ap, in_=i_ap, func=AF.Exp, bias=SM[:, em[0]:em[0] + 1])
                S.add("act", ex, reads=[*VR(b), "c_mask", "c_zero", "c_lohi"], writes=[("PT", ti)])
            else:
                for par in range(2):
                    o_ap = PT[:, ti, :].rearrange("p (h t) -> p h t", t=128)[:, :, par * 64:(par + 1) * 64]
                    if L2:
                        i_ap = PSB[b][:, par * 256:(par + 1) * 256].rearrange("p (h t) -> p h t", t=64)
                    else:
                        i_ap = PSB[b][:, :].rearrange("p (h t) -> p h t", t=128)[:, :, par * 64:(par + 1) * 64]
                    def ex2(e, par=par):
                        o_ap = PT[:, ti, :].rearrange("p (h t) -> p h t", t=128)[:, :, par * 64:(par + 1) * 64]
                        if L2:
                            i_ap = PSB[b][:, par * 256:(par + 1) * 256].rearrange("p (h t) -> p h t", t=64)
                        else:
                            i_ap = PSB[b][:, :].rearrange("p (h t) -> p h t", t=128)[:, :, par * 64:(par + 1) * 64]
                        return e.activation(out=o_ap, in_=i_ap, func=AF.Exp, bias=SM[:, em[par]:em[par] + 1])
                    S.add("act", ex2, reads=[*VR(b), "c_mask", "c_zero", "c_lohi"], writes=[("PT", ti)])

        def pt_lhsT(kt, ti, hl, q0, q1, k0, k1):
            return PT[k0:k1, ti, hl * 128 + q0:hl * 128 + q1]

        GA = GAS[qs]
        tblocks = []
        for ti, kt in enumerate(key_tiles):
            bl = []
            for (q0, q1, k0, k1) in kt["blocks"]:
                q0c, q1c = max(q0, r0), min(q1, r1)
                if q0c < q1c:
                    bl.append((q0c, q1c, k0, k1))
            tblocks.append(bl)
        assert all(len(bl) == 1 and bl[0][0] == r0 and bl[0][1] == r1 for bl in tblocks), "one block per tile covering all rows"

        po_of = {}

        def POV(hg):
            if hg not in po_of:
                po_of[hg] = new_vb("po")
            return po_of[hg]

        def add_pv_tile(hg, ti):
            kt = key_tiles[ti]
            (q0, q1, k0, k1) = tblocks[ti][0]
            pov = POV(hg)

            def pv(e):
                ins = None
                for hl in range(4):
                    head = hg * 4 + hl
                    ins = e.matmul(PA[pov][q0:q1, hl * 65:(hl + 1) * 65], lhsT=pt_lhsT(kt, ti, hl, q0, q1, k0, k1),
                                   rhs=kt["v"](head, k0, k1), start=(ti == 0 and hl == 0), stop=(ti == nkt - 1),
                                   skip_group_check=True)
                return ins
            S.add("pe", pv, reads=[("PT", ti), kt["vres"]], writes=VR(pov))

        def add_norm(hg):
            pov = POV(hg)
            den = lambda: PA[pov][r0:r1, 0:260].rearrange("p (h c) -> p h c", c=65)[:, :, 64:65]
            oin = lambda: PA[pov][r0:r1, 0:260].rearrange("p (h c) -> p h c", c=65)[:, :, 0:64]
            if l == 0:
                S.add("dve", lambda e: e.tensor_tensor(out=SM[r0:r1, C_DEN:C_DEN + 4].rearrange("p (h c) -> p h c", c=1), in0=den(),
                                                       in1=ESBC[r0:r1, hg * 4:(hg + 1) * 4].rearrange("p (h c) -> p h c", c=1),
                                                       op=ALU.add), reads=[*VR(pov), "esbc"], writes=["c_den"])
                S.add("dve", lambda e: e.reciprocal(out=SM[r0:r1, C_RDEN:C_RDEN + 4], in_=SM[r0:r1, C_DEN:C_DEN + 4]),
                      reads=["c_den"], writes=["c_rden"])
            else:
                S.add("dve", lambda e: e.reciprocal(out=SM[r0:r1, C_RDEN:C_RDEN + 4].rearrange("p (h c) -> p h c", c=1), in_=den()),
                      reads=VR(pov), writes=["c_rden"])
            S.add("dve", lambda e: e.tensor_tensor(out=OB[r0:r1, hg * 256:(hg + 1) * 256].rearrange("p (h d) -> p h d", d=64),
                                                   in0=oin(), in1=bc3(SM[r0:r1, C_RDEN:C_RDEN + 4], 4, 64), op=ALU.mult),
                  reads=[*VR(pov), "c_rden"], writes=[("ob", hg)])
            S.add("dve", lambda e: e.tensor_tensor(out=OB[r0:r1, hg * 256:(hg + 1) * 256], in0=OB[r0:r1, hg * 256:(hg + 1) * 256],
                                                   in1=GA[r0:r1, hg * 256:(hg + 1) * 256], op=ALU.mult),
                  reads=[("ob", hg), ("ga", qs, hg // 2)], writes=[("ob", hg)])

        early = min(2, nkt)
        pre = {}
        qk_reads = lambda kt: [kt["kres"], "adm", f"dn1_{l}", f"dn2_{l}"] + qt_res(qs)
        for hg in range(4):
            for ti, kt in enumerate(key_tiles):
                if (hg, ti) in pre:
                    b = pre[(hg, ti)]
                else:
                    b = next_psb()
                    S.add("pe", mk_qk(hg, ti, kt, b), reads=qk_reads(kt), writes=VR(b))
                add_exp(ti, kt, b)
                if ti >= 1:
                    add_pv_tile(hg, ti - 1)
            if hg < 3:
                for ti in range(early):
                    b = next_psb()
                    pre[(hg + 1, ti)] = b
                    S.add("pe", mk_qk(hg + 1, ti, key_tiles[ti], b), reads=qk_reads(key_tiles[ti]), writes=VR(b))
            add_pv_tile(hg, nkt - 1)
            add_norm(hg)

    def out_proj_residual(l, xs, qs=0):
        cfg = layer_cfg(l)
        WO, WOres = cfg["WO"], cfg["WOres"]
        vt = new_vb("tr")

        def tr(e):
            ins = None
            for k in range(8):
                ins = e.transpose(out=PTRS[vt][:, k * 128:(k + 1) * 128], in_=OB[:, k * 128:(k + 1) * 128], identity=IDENT[:])
            return ins
        S.add("pe", tr, reads=[("ob", h_) for h_ in range(4)] + ["ident"], writes=VR(vt))
        S.add("act", lambda e: e.activation(out=OGT[:].rearrange("p k t -> p (k t)"), in_=PTRS[vt][:], func=AF.Copy),
              reads=VR(vt), writes=[("ogT", k) for k in range(8)])
        X = XS[xs]
        for j in range(2):
            vy = new_vb("y")

            def mm(e, j=j, vy=vy):
                ins = None
                for k in range(8):
                    ins = e.matmul(PA[vy][:, 0:512], lhsT=OGT[:, k, :], rhs=WO[:, k, j * 512:(j + 1) * 512],
                                   start=(k == 0), stop=(k == 7), skip_group_check=True)
                return ins
            S.add("pe", mm, reads=[("ogT", k) for k in range(8)] + [(WOres, 0, k) for k in range(8)], writes=VR(vy))
            S.add("dve", lambda e, j=j, vy=vy: e.tensor_tensor(out=X[:, j * 512:(j + 1) * 512], in0=X[:, j * 512:(j + 1) * 512],
                                                               in1=PA[vy][:, 0:512], op=ALU.add),
                  reads=[("x", xs), *VR(vy)], writes=[("x", xs)])

    def emcols(mask, edge):
        base = C_MASK if mask else C_ZERO
        if edge == "prev":
            return (base, C_LOH if mask else C_LO)
        if edge == "cur":
            return (C_HIH if mask else C_HI, base)
        return (base, base)

    def ktA(slot, bias, blocks, mask, edge=None):
        return dict(kt=lambda pair, slot=slot: KTA[slot][:, pair // 4, :], kres=("KTA", slot),
                    v=lambda head, k0, k1, slot=slot: VA[slot][k0:k1, head // 8, :], vres=("VA", slot),
                    bias=bias, blocks=blocks, em=emcols(mask, edge))

    def ktB(slot, bias, blocks, mask, edge=None):
        return dict(kt=lambda pair, slot=slot: KTB[slot][:, pair, :], kres=("KTB", slot),
                    v=lambda head, k0, k1, slot=slot: VB[slot][k0:k1, head, :], vres=("VB", slot),
                    bias=bias, blocks=blocks, em=emcols(mask, edge))

    setup_ops = S.cur
    S.cur = []
    qs_of = {}
    qctr = [0]

    def new_qs(key):
        qs_of[key] = qctr[0] % NQS
        qctr[0] += 1
        return qs_of[key]

    IP = in_proj2 if HALF else in_proj
    SQS = sq_store2 if HALF else sq_store

    def gen_phase(gi, ph):
        S.cur = []
        sample = (gi == NU)
        xs = gi % 2
        is_halo = gi < NH
        m = gi - NH
        if not sample:
            if ph == "A12":
                kdst = vdst = None
                if gi == NU - 1:
                    kdst = SQS([(kap[:, :], 0, 128)])
                    vdst = SQS([(vap[:, :], 0, 128)])
                IP(0, xs, gi % 3, None, kdst, vdst, need_q=(gi >= 1), qs=new_qs((gi, 0)) if gi >= 1 else 0)
            elif ph == "A34":
                if gi >= 1:
                    tiles = [ktA((gi - 1) % 3, BIAS_PREV, [(0, 128, 0, 128)], (gi - 1) < NH, "prev"),
                             ktA(gi % 3, BIAS_CUR, [(0, 128, 0, 128)], is_halo, "cur")]
                    attention(0, tiles, (0, 128), qs=qs_of[(gi, 0)])
                    out_proj_residual(0, xs, qs=qs_of[(gi, 0)])
            elif ph == "B12":
                if gi >= 1:
                    kdst = vdst = None
                    if (not is_halo) and m >= nm - 4:
                        j = m - (nm - 4)
                        kdst = SQS([(kbp[j * 128:(j + 1) * 128, :], 0, 128)])
                        vdst = SQS([(vbp[j * 128:(j + 1) * 128, :], 0, 128)])
                    IP(1, xs, None, gi % NKB, kdst, vdst, need_q=(not is_halo),
                            qs=new_qs((gi, 1)) if not is_halo else 0)
            elif ph == "B34":
                if not is_halo:
                    tiles = []
                    for t in range(gi - 4, gi + 1):
                        blocks, edge = [(0, 128, 0, 128)], None
                        if t == gi - 4:
                            bias, edge = [], "prev"
                        elif t == gi - 1:
                            bias = BIAS_PREV
                        elif t == gi:
                            bias, edge = BIAS_CUR, "cur"
                        else:
                            bias = []
                        tiles.append(ktB(t % NKB, bias, blocks, t < NH, edge))
                    attention(1, tiles, (0, 128), qs=qs_of[(gi, 1)])
                    out_proj_residual(1, xs, qs=qs_of[(gi, 1)])
                    dma("sp", y_p[m * 128:(m + 1) * 128, :], XS[xs][:], [("x", xs)], [], f"y{xs}")
        else:
            OGF = OGT[:].rearrange("p k t -> p (k t)")
            own_a = NU % 3
            ca = [(NU + 1) % 3, (NU + 2) % 3]
            own_b = NU % NKB
            cb = [q for q in range(NKB) if q != own_b][:4]
            if ph == "A12":
                kdst = SQS([(kas[0, 64:128, :], 0, 64), (kas[1, 64:128, :], 64, 128)])
                vdst = SQS([(vas[0, 64:128, :], 0, 64), (vas[1, 64:128, :], 64, 128)])
                IP(0, xs, own_a, None, kdst, vdst, need_q=True, qs=new_qs((gi, 0)))
            elif ph == "A34":
                for s_ in range(2):
                    for u in range(2):
                        dst = OGF[:, 0:256].rearrange("p (g u d) -> p g u d", g=2, u=2)[:, :, u, :]
                        dma("pool", dst, cka[s_, :, :].rearrange("p (g d) -> p g d", d=64), [], [("ogT", k_) for k_ in range(8)], "cache")

                    vc = new_vb("tr")

                    def trc(e, vc=vc):
                        ins = None
                        for g in range(2):
                            ins = e.transpose(out=PTRS[vc][:, g * 128:(g + 1) * 128], in_=OGF[:, g * 128:(g + 1) * 128], identity=IDENT[:])
                        return ins
                    S.add("pe", trc, reads=[("ogT", k_) for k_ in range(8)] + ["ident"], writes=VR(vc))
                    S.add("act", lambda e, s_=s_, vc=vc: e.activation(out=KTA[ca[s_]][:].rearrange("p g t -> p (g t)"), in_=PTRS[vc][:, 0:256], func=AF.Copy),
                          reads=VR(vc), writes=[("KTA", ca[s_])])
                    dma("pool", VA[ca[s_]][:, :, 0:64], cva[s_, :, :].rearrange("p (g d) -> p g d", d=64), [], [("VA", ca[s_])], f"cva{s_}")
                for s_ in range(2):
                    rows = (s_ * 64, (s_ + 1) * 64)
                    tiles = [ktA(ca[s_], BIAS_CACHE(s_), [(rows[0], rows[1], 0, 128)], False),
                             ktA(own_a, BIAS_CUR, [(rows[0], rows[1], rows[0], rows[1])], False)]
                    attention(0, tiles, rows, qs=qs_of[(gi, 0)])
                out_proj_residual(0, xs, qs=qs_of[(gi, 0)])
            elif ph == "B12":
                kdst = SQS([(kbs[0, 448:512, :], 0, 64), (kbs[1, 448:512, :], 64, 128)])
                vdst = SQS([(vbs[0, 448:512, :], 0, 64), (vbs[1, 448:512, :], 64, 128)])
                IP(1, xs, None, own_b, kdst, vdst, need_q=True, qs=new_qs((gi, 1)))
            elif ph == "B34":
                for s_ in range(2):
                    rows = (s_ * 64, (s_ + 1) * 64)
                    for t in range(4):
                        dma("pool", OGF, ckb[s_, t * 128:(t + 1) * 128, :], [], [("ogT", k_) for k_ in range(8)], "cache")

                        vc = new_vb("tr")

                        def trc(e, vc=vc):
                            ins = None
                            for k in range(8):
                                ins = e.transpose(out=PTRS[vc][:, k * 128:(k + 1) * 128], in_=OGF[:, k * 128:(k + 1) * 128], identity=IDENT[:])
                            return ins
                        S.add("pe", trc, reads=[("ogT", k_) for k_ in range(8)] + ["ident"], writes=VR(vc))
                        S.add("act", lambda e, t=t, vc=vc: e.activation(out=KTB[cb[t]][:].rearrange("p k t -> p (k t)"), in_=PTRS[vc][:], func=AF.Copy),
                              reads=VR(vc), writes=[("KTB", cb[t])])
                        dma("pool", VB[cb[t]][:, :, 0:64], cvb[s_, t * 128:(t + 1) * 128, :].rearrange("p (h d) -> p h d", d=64), [], [("VB", cb[t])], f"cv{t}")
                    tiles = []
                    for t in range(4):
                        tiles.append(ktB(cb[t], BIAS_CACHE(s_) if t == 3 else [], [(rows[0], rows[1], 0, 128)], False))
                    tiles.append(ktB(own_b, BIAS_CUR, [(rows[0], rows[1], rows[0], rows[1])], False))
                    attention(1, tiles, rows, qs=qs_of[(gi, 1)])
                out_proj_residual(1, xs, qs=qs_of[(gi, 1)])
                dma("sp", y_s[:, :], XS[xs][:], [("x", xs)], [], f"y{xs}")
        ops = S.cur
        S.cur = []
        return ops

    def gen_xload(g):
        S.cur = []
        if g is not None:
            if g < NU:
                dma("sp", XS[g % 2][:], xin[g * 128:(g + 1) * 128, :], [], [("x", g % 2)], f"x{g % 2}")
            else:
                dma("sp", XS[g % 2][:], xs_in[:, :], [], [("x", g % 2)], f"x{g % 2}")
        ops = S.cur
        S.cur = []
        return ops

    TL = _Timeline()
    tl_end = [0.0]

    def tl_commit_list(lst):
        for rec in lst:
            tl_end[0] = max(tl_end[0], TL.commit(rec))
        return lst

    def merge(a, b):
        out = []
        i = j = 0
        na, nb = len(a), len(b)
        lane_end = [0.0, 0.0]
        t_begin = tl_end[0]
        while i < na or j < nb:
            if j >= nb:
                pick = 0
            elif i >= na:
                pick = 1
            elif GREEDY:
                ta, tb = TL.start_time(a[i]), TL.start_time(b[j])
                if abs(ta - tb) < 1.0:
                    pick = 0 if i * nb <= j * na else 1
                else:
                    pick = 0 if ta < tb else 1
            else:
                pick = 0 if i * nb <= j * na else 1
            rec = a[i] if pick == 0 else b[j]
            if pick == 0:
                i += 1
            else:
                j += 1
            fin = TL.commit(rec)
            lane_end[pick] = max(lane_end[pick], fin)
            tl_end[0] = max(tl_end[0], fin)
            out.append(rec)
        if _os0.environ.get("KTLV"):
            pe_a = sum(TL._cost(r)[0] for r in a if r[0] == "pe") / 1e3
            pe_b = sum(TL._cost(r)[0] for r in b if r[0] == "pe") / 1e3
            print("SLOT t0=%.1f lane12: n=%d end=%.1f pe=%.1f | lane34: n=%d end=%.1f pe=%.1f" % (t_begin / 1e3, na, lane_end[0] / 1e3, pe_a, nb, lane_end[1] / 1e3, pe_b))
        return out

    units = list(range(NU)) + ([NU] if do_sample else [])

    def gen_cpy(after=()):
        S.cur = []
        if do_sample:
            for s_ in range(2):
                dma("sp", kas[s_, 0:64, :], cka[s_, 64:128, :], list(after), [], "cpy")
                dma("sp", vas[s_, 0:64, :], cva[s_, 64:128, :], list(after), [], "cpy")
                dma("sp", kbs[s_, 0:448, :], ckb[s_, 64:512, :], list(after), [], "cpy")
                dma("sp", vbs[s_, 0:448, :], cvb[s_, 64:512, :], list(after), [], "cpy")
        ops = S.cur
        S.cur = []
        return ops

    if DAGSCHED:
        sem = list(setup_ops)
        cpy_unit = min(10, NU - 1)
        for i_u, u in enumerate(units):
            sem += gen_xload(u)
            for ph in ("A12", "A34", "B12", "B34"):
                sem += gen_phase(u, ph)
                if i_u == 0 and ph == "A12":
                    sem += late_weights(0, after=[("KTA", u % 3)])
                if i_u == 1 and ph == "B12":
                    sem += late_weights(1, after=[("KTB", u % NKB)])
                if u == cpy_unit and ph == "B12":
                    sem += gen_cpy(after=[("KTB", u % NKB)])
        n_ops = len(sem)
        _free = set(_os0.environ.get("KFREE", "").split(",")) - {""}
        if _free:
            ver = {}
            ren = []
            base = lambda r: (r[0] if isinstance(r, tuple) else r)
            for (eng, fn, reads, writes, dsem) in sem:
                rd = [((r, ver.get(r, 0)) if base(r) in _free else r) for r in reads]
                wr = []
                for w in writes:
                    if base(w) in _free:
                        if w not in reads:
                            ver[w] = ver.get(w, 0) + 1
                        wr.append((w, ver.get(w, 0)))
                    else:
                        wr.append(w)
                ren.append((eng, fn, rd, wr, dsem))
            sem = ren
        lastw, readers = {}, {}
        preds = [set() for _ in range(n_ops)]
        for i, (eng, fn, reads, writes, dsem) in enumerate(sem):
            for r in reads:
                if r in lastw:
                    preds[i].add(lastw[r])
            for w in writes:
                if w in lastw:
                    preds[i].add(lastw[w])
                for q in readers.get(w, ()):
                    preds[i].add(q)
            preds[i].discard(i)
            for r in reads:
                readers.setdefault(r, []).append(i)
            for w in writes:
                lastw[w] = i
                readers[w] = []
        succs = [[] for _ in range(n_ops)]
        indeg = [len(p) for p in preds]
        for i, p in enumerate(preds):
            for q in p:
                succs[q].append(i)
        ready = set(i for i in range(n_ops) if indeg[i] == 0)
        order = []
        blev = [0.0] * n_ops
        for i in range(n_ops - 1, -1, -1):
            c = TL._cost(sem[i])[0]
            blev[i] = c + max([blev[j] for j in succs[i]], default=0.0)

        def vbs_of(rec):
            out = []
            for r in list(rec[2]) + list(rec[3]):
                if isinstance(r, tuple) and r[0] == "vb" and r[1] not in out:
                    out.append(r[1])
            return out
        remaining = {}
        for rec in sem:
            for v in vbs_of(rec):
                remaining[v] = remaining.get(v, 0) + 1
        holder = [None] * 8

        def bank_free(bk):
            return holder[bk] is None or remaining[holder[bk]] == 0

        def renamed(rec, extra=None):
            m = dict(bind)
            if extra:
                m.update(extra)
            ren = lambda r: (("pb", m[r[1]], r[2]) if (isinstance(r, tuple) and r[0] == "vb") else r)
            return (rec[0], rec[1], [ren(r) for r in rec[2]], [ren(r) for r in rec[3]], rec[4])

        def choose_banks(rec):
            new_v = [v for v in vbs_of(rec) if v not in bind]
            if not new_v:
                return {}
            free = [bk for bk in range(8) if bank_free(bk)]
            need_spare = 1 if any(vb_kind[v] == "proj" for v in new_v) else 0
            if len(free) < len(new_v) + need_spare:
                return None
            chosen = {}
            for v in new_v:
                bestb, bt = None, None
                for bk in free:
                    if bk in chosen.values():
                        continue
                    t_ = max([TL.w_t.get(("pb", bk, p_), (0.0, None))[0] for p_ in (0, 1)] +
                             [ft for p_ in (0, 1) for (ft, fe) in TL.r_t.get(("pb", bk, p_), ())] + [0.0])
                    if bt is None or t_ < bt:
                        bestb, bt = bk, t_
                chosen[v] = bestb
            return chosen

        while ready:
            best, bkey, bch = None, None, None
            for i in ready:
                ch = choose_banks(sem[i])
                if ch is None:
                    continue
                st = TL.start_time(renamed(sem[i], ch))
                if PRIO == "blevel":
                    key = (int(st // SCHED_Q), -blev[i], i)
                else:
                    key = (int(st // SCHED_Q), i)
                if bkey is None or key < bkey:
                    best, bkey, bch = i, key, ch
            if best is None:
                hk = [(vb_kind[holder[bk]], remaining[holder[bk]]) if holder[bk] is not None else None for bk in range(8)]
                rk = [[vb_kind[v] for v in vbs_of(sem[i]) if v not in bind] for i in ready]
                print("ALLOC DEADLOCK banks:", hk, "ready needs:", rk[:20], "n_ready", len(ready), "emitted", len(order))
            assert best is not None, "PSUM bank allocation deadlock in the list scheduler"
            picks = [best]
            if sem[best][4] is not None and sem[best][4] in GROUP_SEMS:
                picks += sorted(i for i in ready if i != best and sem[i][4] is sem[best][4])
            for bi in picks:
                ready.discard(bi)
                ch = bch if bi == best else choose_banks(sem[bi])
                for v, bk in (ch or {}).items():
                    bind[v] = bk
                    holder[bk] = v
                rec = renamed(sem[bi])
                for v in vbs_of(sem[bi]):
                    remaining[v] -= 1
                tl_end[0] = max(tl_end[0], TL.commit(rec))
                order.append(rec)
                for j in succs[bi]:
                    indeg[j] -= 1
                    if indeg[j] == 0:
                        ready.add(j)
        assert len(order) == n_ops
    else:
        order = tl_commit_list(list(setup_ops))
        order += tl_commit_list(gen_cpy())
        if PIPELINE:
            L12 = [("x", units[0]), ("x", units[1]) if len(units) > 1 else None, ("p", units[0], "A12"), ("w", 0)]
            L34 = []
            k = 0
            while k < len(units):
                ua = units[k]
                ub = units[k + 1] if k + 1 < len(units) else None
                un = units[k + 2] if k + 2 < len(units) else None
                un2 = units[k + 3] if k + 3 < len(units) else None
                L12 += [("p", ub, "A12"), ("p", ua, "B12")]
                if k == 0:
                    L12 += [("w", 1)]
                L12 += [("p", ub, "B12"), ("x", un), ("p", un, "A12"), ("x", un2)]
                L34 += [("p", ua, "A34"), ("p", ub, "A34"), ("p", ua, "B34"), ("p", ub, "B34")]
                k += 2
            L12 = [p for p in L12 if p is not None and p[1] is not None]
            L34 = [p for p in L34 if p[1] is not None]
            ops_of = {}
            prereq = {}
            slot_reader = {}
            for p in L12:
                if p[0] == "x":
                    ops_of[p] = gen_xload(p[1])
                    prereq[p] = [("p", p[1] - 2, "B34")]
                elif p[0] == "w":
                    ops_of[p] = late_weights(p[1])
                    prereq[p] = []
                else:
                    _, u, ph = p
                    ops_of[p] = gen_phase(u, ph)
                    pr = []
                    if ph == "A12":
                        pr.append(("p", u - 2, "A34"))
                    else:
                        pr += [("p", u, "A34"), ("p", u - 2, "B34")]
                    key = (u, 0 if ph == "A12" else 1)
                    if key in qs_of:
                        q_ = qs_of[key]
                        if q_ in slot_reader:
                            pr.append(slot_reader[q_])
                        slot_reader[q_] = ("p", u, "A34" if ph == "A12" else "B34")
                    prereq[p] = pr
            for p in L34:
                _, u, ph = p
                ops_of[p] = gen_phase(u, ph)
                prereq[p] = [("p", u, "A12" if ph == "A34" else "B12")]
            lanes = [L12, L34]
            pos = [0, 0]
            opi = [0, 0]
            done = set()
            allp = set(L12) | set(L34)

            def head(li):
                while pos[li] < len(lanes[li]):
                    p = lanes[li][pos[li]]
                    if opi[li] == 0 and any((q in allp) and (q not in done) for q in prereq[p]):
                        return None
                    if opi[li] < len(ops_of[p]):
                        return ops_of[p][opi[li]]
                    done.add(p)
                    pos[li] += 1
                    opi[li] = 0
                return None

            while True:
                progressed = True
                while progressed:
                    before = (tuple(pos), len(done))
                    h0, h1 = head(0), head(1)
                    progressed = (tuple(pos), len(done)) != before
                if h0 is None and h1 is None:
                    break
                if h1 is None:
                    pick = 0
                elif h0 is None:
                    pick = 1
                elif GREEDY:
                    pick = 0 if TL.start_time(h0) <= TL.start_time(h1) + LANE_BIAS else 1
                else:
                    pick = 0 if len(order) % 2 == 0 else 1
                rec = h0 if pick == 0 else h1
                opi[pick] += 1
                tl_end[0] = max(tl_end[0], TL.commit(rec))
                order.append(rec)
            assert pos[0] == len(L12) and pos[1] == len(L34), ("pipeline gating deadlock", pos, len(L12), len(L34))
        else:
            order += late_weights(0) + late_weights(1)
            for i_u, u in enumerate(units):
                if i_u == 0:
                    order += gen_xload(u)
                for ph in ("A12", "A34", "B12", "B34"):
                    order += gen_phase(u, ph)
                    if ph == "A12" and i_u + 1 < len(units):
                        order += gen_xload(units[i_u + 1])
    if _os0.environ.get("KTL"):
        print("TIMELINE predicted end: %.1f us; engine busy-until: %s" % (tl_end[0] / 1e3, {k: round(v / 1e3) for k, v in TL.busy.items()}))
    S.resolve(order)

    import os as _os
    _tr = int(_os.environ.get("KTRUNC", "0"))
    if _os.environ.get("KMARKS"):
        print("MARKS", marks, "total", len(S.ops))
        pass
    if _tr > 0:
        S.ops = S.ops[:_tr]
    S.finalize(engsems, group_sems=GROUP_SEMS,
               burst_sems=[dsems["xt"], dsems["ogt"]])
    finals = [(s, v) for s, v in S.final.items() if s in dsems.values()]
    with nc.Block() as block:
        @block.sync
        def _(e):
            S.emit("sp", e, extra_final=finals)

        @block.gpsimd
        def _(e):
            S.emit("pool", e)

        @block.scalar
        def _(e):
            S.emit("act", e)

        @block.vector
        def _(e):
            S.emit("dve", e)

        @block.tensor
        def _(e):
            S.emit("pe", e)
    for cm in reversed(ctx):
        cm.__exit__(None, None, None)
    return nc


_CACHE = {}


def _get_nc(nm, do_sample):
    key = (nm, do_sample)
    if key not in _CACHE:
        _CACHE[key] = build(nm, do_sample)
    return _CACHE[key]


def kernel(x_prompt, x_sample, cache_k_a, cache_v_a, cache_k_b, cache_v_b,
           t5_table, norm_a, w_in_a, q_norm_a, k_norm_a, sinks_a, w_out_a,
           norm_b, w_in_b, q_norm_b, k_norm_b, rel_bias_b, w_out_b, _nm=NM, _do_sample=True):
    f = lambda a: np.ascontiguousarray(np.asarray(a, dtype=np.float32))
    nm = _nm
    xp = f(x_prompt)[0]
    xsamp = f(x_sample)
    oha, ohb, ident, adm = _consts()
    rbT = np.zeros((384, 16), np.float32)
    rbT[:257] = f(rel_bias_b).T
    rbT = rbT.reshape(3, 128, 16)
    tok_per_core = 2048
    in_maps = []
    for c in range(NCORES):
        start = c * tok_per_core - NH * 128
        n = (NH + nm) * 128
        xin = np.zeros((n, D), np.float32)
        lo = max(start, 0)
        xin[lo - start:] = xp[lo:start + n]
        mask = np.full((128, 1), MASKV if c == 0 else 0.0, np.float32)
        in_maps.append({
            "xin": xin,
            "xs": xsamp[2 * c:2 * c + 2].reshape(128, D),
            "cka": f(cache_k_a)[2 * c:2 * c + 2].reshape(2, 128, 128),
            "cva": f(cache_v_a)[2 * c:2 * c + 2].reshape(2, 128, 128),
            "ckb": f(cache_k_b)[2 * c:2 * c + 2].reshape(2, 512, 1024),
            "cvb": f(cache_v_b)[2 * c:2 * c + 2].reshape(2, 512, 1024),
            "t5": f(t5_table), "rbT": rbT, "oha": oha, "ohb": ohb, "ident": ident, "adm": adm, "maskv": mask,
            "norm_a": f(norm_a), "norm_b": f(norm_b), "q_norm_a": f(q_norm_a), "k_norm_a": f(k_norm_a),
            "q_norm_b": f(q_norm_b), "k_norm_b": f(k_norm_b), "sinks": f(sinks_a),
            "w_in_a": f(w_in_a), "w_out_a": f(w_out_a), "w_in_b": f(w_in_b), "w_out_b": f(w_out_b),
        })
    nc = _get_nc(nm, _do_sample)
    res = run_bass_kernel_spmd(nc, in_maps, core_ids=list(range(NCORES)))
    R = res.results
    y_prompt = np.zeros((1, 16384, D), np.float32)
    for c in range(NCORES):
        y_prompt[0, c * tok_per_core:c * tok_per_core + nm * 128] = R[c]["y_p"]
    y_sample = np.stack([R[c]["y_s"].reshape(2, 64, D) for c in range(NCORES)]).reshape(16, 64, D)
    last = R[NCORES - 1]
    k_a_p = last["kap"].reshape(1, 128, 2, 64)
    v_a_p = last["vap"].reshape(1, 128, 2, 64)
    k_b_p = last["kbp"].reshape(1, 512, 16, 64)
    v_b_p = last["vbp"].reshape(1, 512, 16, 64)
    cat = lambda k, shp: np.concatenate([R[c][k] for c in range(NCORES)], axis=0).reshape(shp)
    k_a_s = cat("kas", (16, 128, 2, 64))
    v_a_s = cat("vas", (16, 128, 2, 64))
    k_b_s = cat("kbs", (16, 512, 16, 64))
    v_b_s = cat("vbs", (16, 512, 16, 64))
    return (y_prompt, y_sample, k_a_p, v_a_p, k_b_p, v_b_p, k_a_s, v_a_s, k_b_s, v_b_s)
```

```python
import numpy as np
import concourse.bass as bass
import concourse.mybir as mybir
from concourse.bass_utils import run_bass_kernel_spmd

F32 = mybir.dt.float32
BF16 = mybir.dt.bfloat16
AF = mybir.ActivationFunctionType
ALU = mybir.AluOpType
AX = mybir.AxisListType

NCORES = 8
import os as _os0
PIPELINE = _os0.environ.get("KPIPE", "1") == "1"
GREEDY = _os0.environ.get("KGREEDY", "1") == "1"
LANE_BIAS = float(_os0.environ.get("KBIAS", "-450"))
HALF = _os0.environ.get("KHALF", "1") == "1"
NPA = int(_os0.environ.get("KNPA", "3"))
NPTR = int(_os0.environ.get("KNPTR", "1"))
NPSB = int(_os0.environ.get("KNPSB", "3"))
POOL = _os0.environ.get("KPOOL", "0") == "1"
DAGSCHED = _os0.environ.get("KDAG", "1") == "1"
SCHED_Q = float(_os0.environ.get("KQ", "300"))
PRIO = _os0.environ.get("KPRIO", "blevel")
NQS = int(_os0.environ.get("KNQS", "1"))
MAXPROJ = int(_os0.environ.get("KMAXPROJ", "4"))
DMA_T = _os0.environ.get("KDMAT", "0") == "1"
INPROJ_ORDER = _os0.environ.get("KORDER", "N Pq Cq Tq Pk Ck Tk Pv Cv Pg Cg").split()
D = 1024
NH = 5
NM = 16
EPS = 1e-6
ONESV = 2.0
MASKV = -30000.0
IN_A = 2304
IN_B = 4096


class _Op:
    __slots__ = ("eng", "fn", "deps", "idx", "dsem", "token", "needed")


class Sched:
    def __init__(self):
        self.ops = []
        self.lastw = {}
        self.readers = {}
        self.cur = []

    EXCL = ("pa", "psb", "ptr", "po", "bk", "vb", "pb")

    def add(self, eng, fn, reads=(), writes=(), dsem=None):
        ex = [r for r in reads if (r[0] if isinstance(r, tuple) else r) in self.EXCL]
        if ex:
            reads = [r for r in reads if r not in ex]
            writes = list(writes) + [r for r in ex if r not in writes]
        self.cur.append((eng, fn, list(reads), list(writes), dsem))

    def resolve(self, order):
        for (eng, fn, reads, writes, dsem) in order:
            deps = set()
            for r in reads:
                if r in self.lastw:
                    deps.add(self.lastw[r])
            for w in writes:
                if w in self.lastw:
                    deps.add(self.lastw[w])
                for q in self.readers.get(w, ()):
                    deps.add(q)
            op = _Op()
            op.eng, op.fn, op.deps, op.dsem = eng, fn, deps, dsem
            op.idx = len(self.ops)
            op.token = None
            op.needed = False
            deps.discard(op.idx)
            for r in reads:
                self.readers.setdefault(r, []).append(op.idx)
            for w in writes:
                self.lastw[w] = op.idx
                self.readers[w] = []
            self.ops.append(op)

    def finalize(self, engsems, group_sems=(), burst_sems=()):
        for op in self.ops:
            for d in op.deps:
                self.ops[d].needed = True
        cnt = {}
        tot = {}
        for op in self.ops:
            if op.dsem is not None:
                tot[op.dsem] = tot.get(op.dsem, 0) + 16
        for op in self.ops:
            if op.dsem is not None:
                cnt[op.dsem] = cnt.get(op.dsem, 0) + 16
                if op.dsem in group_sems:
                    v = tot[op.dsem]
                elif op.dsem in burst_sems:
                    v = ((cnt[op.dsem] + 127) // 128) * 128
                else:
                    v = cnt[op.dsem]
                op.token = (op.dsem, v)
            elif op.needed:
                s = engsems[op.eng]
                cnt[s] = cnt.get(s, 0) + 1
                op.token = (s, cnt[s])
        self.final = cnt

    def emit(self, engname, eng, extra_final=()):
        waited = {}
        for op in self.ops:
            if op.eng != engname:
                continue
            need = {}
            for d in op.deps:
                dop = self.ops[d]
                if dop.eng == "pe" and engname == "pe" and dop.dsem is None:
                    continue
                s, v = dop.token
                if need.get(s, 0) < v:
                    need[s] = v
            for s, v in need.items():
                if waited.get(s, 0) < v:
                    eng.wait_ge(s, v)
                    waited[s] = v
            ins = op.fn(eng)
            if op.token is not None:
                ins.then_inc(op.token[0], 16 if op.dsem is not None else 1)
        for s, v in extra_final:
            eng.wait_ge(s, v)


import os as _os1
CM = [float(v) for v in _os1.environ.get("KCM", "60,220,150,1.04").split(",")]
SYNCV = float(_os1.environ.get("KSYNC", "450"))


class _FakeIns:
    def then_inc(self, *a, **k):
        return self


class _FakeEng:
    def __init__(self, engname):
        self.engname = engname
        self.t = 0.0
        self.dma = False

    def __getattr__(self, name):
        def f(*args, **kw):
            out = kw.get("out", args[0] if args else None)
            n = 64
            if out is not None and hasattr(out, "shape"):
                n = 1
                for d in list(out.shape)[1:]:
                    n *= int(d)
            if name == "dma_start":
                self.dma = True
                self.t += 2500.0 + 2.0 * n
            elif name in ("matmul", "transpose"):
                self.t += CM[0] + 0.42 * n
            elif self.engname == "act":
                self.t += CM[1] + 0.83 * n
            elif self.engname == "pool":
                self.t += 1700.0 if n <= 64 else 100.0 + 2.0 * n
            else:
                self.t += CM[2] + CM[3] * n
            return _FakeIns()
        return f


def _op_cost(rec):
    fe = _FakeEng(rec[0])
    rec[1](fe)
    return fe.t, fe.dma


class _Timeline:
    SYNC = SYNCV

    def __init__(self):
        self.eng_free = {}
        self.w_t = {}
        self.r_t = {}
        self.cost = {}
        self.busy = {}

    def _cost(self, rec):
        k = id(rec[1])
        if k not in self.cost:
            self.cost[k] = _op_cost(rec)
        return self.cost[k]

    def start_time(self, rec):
        eng, fn, reads, writes, dsem = rec
        t = self.eng_free.get(eng, 0.0)
        for r in reads:
            if r in self.w_t:
                ft, fe = self.w_t[r]
                t = max(t, ft + (self.SYNC if fe != eng else 0.0))
        for w in writes:
            if w in self.w_t:
                ft, fe = self.w_t[w]
                t = max(t, ft + (self.SYNC if fe != eng else 0.0))
            for (ft, fe) in self.r_t.get(w, ()):
                t = max(t, ft + (self.SYNC if fe != eng else 0.0))
        return t

    def commit(self, rec):
        eng, fn, reads, writes, dsem = rec
        st = self.start_time(rec)
        c, is_dma = self._cost(rec)
        if is_dma:
            self.eng_free[eng] = st + (900.0 if eng == "pool" else 100.0)
            fin, fe = st + c, "dma"
        else:
            self.eng_free[eng] = st + c
            fin, fe = st + c, eng
            self.busy[eng] = self.busy.get(eng, 0.0) + c
        for r in reads:
            self.r_t.setdefault(r, []).append((fin, fe))
        for w in writes:
            self.w_t[w] = (fin, fe)
            self.r_t[w] = []
        return fin

def _t5_bucket(rel):
    nb = 16
    max_exact = 8
    ret = np.where(rel > 0, nb, 0)
    n = np.abs(rel)
    nf = np.maximum(n, 1).astype(np.float32)
    large = max_exact + (np.log(nf / max_exact) / np.float32(np.log(128 / max_exact))
                         * (nb - max_exact)).astype(np.int32)
    large = np.minimum(large, nb - 1)
    return ret + np.where(n < max_exact, n, large)


def _consts():
    n = np.arange(256)
    relA = 63 - n
    bA = _t5_bucket(relA.astype(np.int32))
    oha = np.zeros((32, 256), np.float32)
    oha[bA, n] = 1.0
    relB = n - 63
    idxB = np.clip(relB, -128, 128) + 128
    ohb = np.zeros((384, 256), np.float32)
    ohb[idxB, n] += 1.0
    ohb[256, :] -= 1.0
    ident = np.eye(128, dtype=np.float32)
    adm = np.zeros((128, 384), np.float32)
    r = np.arange(128)
    adm[r, 255 - r] = 1.0
    return oha, ohb.reshape(3, 128, 256), ident, adm


def build(nm=NM, do_sample=True):
    nc = bass.Bass("TRN2", target_bir_lowering=False)
    NU = NH + nm

    def din(name, shape):
        return nc.dram_tensor(name, list(shape), F32, kind="ExternalInput").ap()

    def dout(name, shape):
        return nc.dram_tensor(name, list(shape), F32, kind="ExternalOutput").ap()

    xin = din("xin", [NU * 128, D])
    xs_in = din("xs", [128, D])
    cka = din("cka", [2, 128, 128])
    cva = din("cva", [2, 128, 128])
    ckb = din("ckb", [2, 512, 1024])
    cvb = din("cvb", [2, 512, 1024])
    t5 = din("t5", [32, 16])
    rbT = din("rbT", [3, 128, 16])
    oha_d = din("oha", [32, 256])
    ohb_d = din("ohb", [3, 128, 256])
    ident_d = din("ident", [128, 128])
    adm_d = din("adm", [128, 384])
    maskv_d = din("maskv", [128, 1])
    norm_a = din("norm_a", [D])
    norm_b = din("norm_b", [D])
    qn_a = din("q_norm_a", [64])
    kn_a = din("k_norm_a", [64])
    qn_b = din("q_norm_b", [64])
    kn_b = din("k_norm_b", [64])
    sinks = din("sinks", [16])
    w_in_a = din("w_in_a", [D, IN_A])
    w_out_a = din("w_out_a", [D, D])
    w_in_b = din("w_in_b", [D, IN_B])
    w_out_b = din("w_out_b", [D, D])

    y_p = dout("y_p", [nm * 128, D])
    y_s = dout("y_s", [128, D])
    kap = dout("kap", [128, 128])
    vap = dout("vap", [128, 128])
    kbp = dout("kbp", [512, 1024])
    vbp = dout("vbp", [512, 1024])
    kas = dout("kas", [2, 128, 128])
    vas = dout("vas", [2, 128, 128])
    kbs = dout("kbs", [2, 512, 1024])
    vbs = dout("vbs", [2, 512, 1024])
    vv_d = nc.dram_tensor("vv_scratch", [2, 16, 256], F32, kind="Internal").ap()

    S = Sched()
    ctx = []
    marks = {}

    def sb(name, shape, dt):
        cm = nc.sbuf_tensor(name, list(shape), dt)
        t = cm.__enter__()
        ctx.append(cm)
        return t

    def ps(name, shape, dt):
        cm = nc.psum_tensor(name, list(shape), dt)
        t = cm.__enter__()
        ctx.append(cm)
        return t

    def sem(name):
        cm = nc.semaphore(name)
        t = cm.__enter__()
        ctx.append(cm)
        return t

    WA = sb("WA", [128, 8, IN_A], BF16)
    WOA = sb("WOA", [128, 8, D], BF16)
    WB = sb("WB", [128, 8, IN_B], BF16)
    WOB = sb("WOB", [128, 8, D], BF16)
    XS = [sb(f"x{i}", [128, D], F32) for i in range(2)]
    NXB = int(_os0.environ.get("KNXB", "2"))
    XNS = [sb(f"xn{i}", [128, D], BF16) for i in range(NXB)]
    XTS = [sb(f"xT{i}", [128, 8, 128], BF16) for i in range(NXB)]
    XN, XT = XNS[0], XTS[0]
    SQ = sb("sq", [128, D], F32)
    QN = sb("qn", [128, D], BF16)
    QTS = [sb(f"qTbd{i}", [128, 8, 2, 128], BF16) for i in range(NQS)]
    NKB = 6
    KTB = [sb(f"kTB{i}", [128, 8, 128], BF16) for i in range(NKB)]
    VB = [sb(f"VB{i}", [128, 16, 65], BF16) for i in range(NKB)]
    KTA = [sb(f"kTA{i}", [128, 2, 128], BF16) for i in range(3)]
    VA = [sb(f"VA{i}", [128, 2, 65], BF16) for i in range(3)]
    PT = sb("PT", [128, 5, 512], BF16)
    GAS = [sb(f"ga{i}", [128, D], BF16) for i in range(NQS)]
    OB = sb("ob", [128, D], BF16)
    OGT = sb("ogT", [128, 8, 128], BF16)
    DN1 = [sb(f"dn1{l}", [128, 16, 64], BF16) for l in range(2)]
    DN2M = sb("dn2m", [128, 16, 64], BF16)
    IDENT = sb("identb", [128, 128], BF16)
    ADM = sb("admb", [128, 384], BF16)
    SM = sb("small", [128, 384], F32)
    GKBC = [sb(f"gkbc{l}", [128, 64], F32) for l in range(2)]
    ESBC = sb("esbc", [128, 16], F32)
    VVS = XS[1][0:16, 0:512].rearrange("p (l n) -> p l n", l=2)
    T5S = XS[1][0:32, 512:528]
    RBS = XS[1][:, 528:576].rearrange("p (k h) -> p k h", k=3)
    GQBC = [XS[1][:, 576 + 64 * l:640 + 64 * l] for l in range(2)]
    OHBS = SQ[:, 0:768].rearrange("p (k n) -> p k n", k=3)
    OHAS = SQ[0:32, 768:1024]

    C_SS = 0
    C_RX = 1
    C_SSQ = 8
    C_RQ = 40
    C_DEN = 72
    C_RDEN = 76
    C_MHALF = 80
    C_ZERO = 112
    C_MASK = 113
    C_GQK = [114, 115]
    C_GQ = [116, 117]
    C_GK = [118, 119]
    C_GN = [120, 128]
    C_LO, C_HI, C_LOH, C_HIH = 184, 185, 186, 187
    C_XST = 176
    C_M64 = 192
    C_TMP = 256

    PHYS = [ps(f"bk{i}", [128, 1024], BF16) for i in range(8)]
    bind = {}
    vb_kind = []

    def new_vb(kind):
        vb_kind.append(kind)
        return len(vb_kind) - 1

    class _Lazy:
        def __init__(self, f):
            self.f = f

        def __getitem__(self, v):
            return self.f(v)

    PA = _Lazy(lambda v: PHYS[bind.get(v, 0)][:].bitcast(F32))
    PSB = PA
    PTRS = _Lazy(lambda v: PHYS[bind.get(v, 0)])
    VR = lambda v: [("vb", v, 0), ("vb", v, 1)]

    engsems = {e: sem("s_" + e) for e in ("pe", "act", "dve", "pool")}
    dsem_names = ["w0", "w1", "w2", "w3", "w4", "w5", "w6", "w7", "w8", "c0", "c1", "x0", "x1", "y0", "y1", "kv", "cache", "cpy", "vv", "dn",
                  "cv0", "cv1", "cv2", "cv3", "cv4", "xt", "ogt", "kv0", "kv1", "cva0", "cva1"]
    dsems = {n: sem("d_" + n) for n in dsem_names}

    GROUP_SEMS = [dsems[n] for n in ("w0", "w1", "w2", "w3", "w4", "w5", "w6", "w7", "w8", "c0", "c1", "vv", "dn", "cpy")]

    def dma(engname, out, in_, reads, writes, dsem, transpose=False, **kw):
        if transpose:
            S.add(engname, lambda e, out=out, in_=in_: e.dma_start_transpose(out=out, in_=in_),
                  reads=reads, writes=writes, dsem=dsems[dsem])
        else:
            S.add(engname, lambda e, out=out, in_=in_, kw=kw: e.dma_start(out=out, in_=in_, **kw),
                  reads=reads, writes=writes, dsem=dsems[dsem])

    dma("pool", IDENT[:], ident_d[:, :], [], ["ident"], "c0")
    dma("pool", ADM[:], adm_d[:, :], [], ["adm"], "c0")
    dma("sp", T5S, t5[:, :], [], ["t5s"], "c1")
    dma("sp", RBS, rbT.rearrange("k p h -> p k h"), [], ["rbs"], "c1")
    dma("sp", OHAS, oha_d[:, :], [], ["ohas"], "c1")
    dma("sp", OHBS, ohb_d.rearrange("k p n -> p k n"), [], ["ohbs"], "c1")
    dma("sp", SM[:, C_MASK:C_MASK + 1], maskv_d[:, :], [], ["c_mask"], "c1")

    def bcast_src(ap1d, n):
        return bass.AP(tensor=ap1d.tensor, offset=ap1d.offset, ap=[[0, 128], [1, n]])

    def col_src(ap1d, n, off=0):
        return bass.AP(tensor=ap1d.tensor, offset=ap1d.offset + off, ap=[[1, n], [1, 1]])

    for l, (qn_, kn_) in enumerate(((qn_a, kn_a), (qn_b, kn_b))):
        dma("sp", GKBC[l][:], bcast_src(kn_, 64), [], [f"gkbc{l}"], "c1")
        dma("sp", GQBC[l], bcast_src(qn_, 64), [], [f"gqbc{l}"], "c1")
    for l, nrm in enumerate((norm_a, norm_b)):
        for k in range(8):
            dma("sp", SM[:, C_GN[l] + k:C_GN[l] + k + 1], col_src(nrm, 128, k * 128), [], [(f"c_gn{l}", k)], "c1")
    dma("sp", ESBC[:], bcast_src(sinks, 16), [], ["esbc"], "c1")

    S.add("dve", lambda e: e.memset(SM[:, C_MHALF:C_MHALF + 32], -0.5), writes=["c_mhalf"])
    S.add("dve", lambda e: e.memset(SM[:, C_ZERO:C_ZERO + 1], 0.0), writes=["c_zero"])
    S.add("dve", lambda e: e.memset(SM[:, C_LO:C_HI + 1], 0.0), writes=["c_lohi"])
    S.add("dve", lambda e: e.memset(SM[0:64, C_LO:C_LO + 1], MASKV), reads=["c_lohi"], writes=["c_lohi"])
    S.add("dve", lambda e: e.memset(SM[64:128, C_HI:C_HI + 1], MASKV), reads=["c_lohi"], writes=["c_lohi"])
    S.add("dve", lambda e: e.tensor_scalar(out=SM[:, C_LOH:C_HIH + 1], in0=SM[:, C_LO:C_HI + 1], scalar1=SM[:, C_MASK:C_MASK + 1],
                                           scalar2=None, op0=ALU.add), reads=["c_lohi", "c_mask"], writes=["c_mask"])
    S.add("dve", lambda e: e.tensor_tensor(out=SM[:, C_M64:C_M64 + 64], in0=IDENT[:, 0:64], in1=IDENT[:, 64:128], op=ALU.add),
          reads=["ident"], writes=["c_m64"])
    for l in range(2):
        S.add("dve", lambda e, l=l: e.tensor_tensor(out=SM[:, C_TMP:C_TMP + 64], in0=GQBC[l], in1=GKBC[l][:], op=ALU.mult),
              reads=[f"gqbc{l}", f"gkbc{l}", ("x", 1)], writes=["c_tmp"])
        S.add("dve", lambda e, l=l: e.tensor_tensor(out=SM[:, C_TMP:C_TMP + 64], in0=SM[:, C_TMP:C_TMP + 64], in1=SM[:, C_M64:C_M64 + 64],
                                                    op=ALU.mult), reads=["c_tmp", "c_m64"], writes=["c_tmp"])
        S.add("dve", lambda e, l=l: e.tensor_reduce(out=SM[:, C_GQK[l]:C_GQK[l] + 1], in_=SM[:, C_TMP:C_TMP + 64], axis=AX.X, op=ALU.add),
              reads=["c_tmp"], writes=[f"c_gqk{l}"])
        S.add("dve", lambda e, l=l: e.tensor_scalar(out=SM[:, C_GQK[l]:C_GQK[l] + 1], in0=SM[:, C_GQK[l]:C_GQK[l] + 1],
                                                    scalar1=0.125, scalar2=None, op0=ALU.mult),
              reads=[f"c_gqk{l}"], writes=[f"c_gqk{l}"])
    S.add("act", lambda e: e.activation(out=ESBC[:], in_=ESBC[:], func=AF.Exp), reads=["esbc"], writes=["esbc"])
    S.add("dve", lambda e: e.tensor_scalar(out=ESBC[:], in0=ESBC[:], scalar1=ONESV, scalar2=None, op0=ALU.mult),
          reads=["esbc"], writes=["esbc"])
    def qt_res(qs_):
        return [("qT", qs_, h_, p_) for h_ in range(2) for p_ in range(2)]

    for i in range(NQS):
        S.add("pool", lambda e, i=i: e.memset(QTS[i][:], 0.0), writes=qt_res(i))
    for i in range(NKB):
        S.add("pool", lambda e, i=i: e.memset(VB[i][:, :, 64:65], ONESV), writes=[("VB", i)])
    for i in range(3):
        S.add("pool", lambda e, i=i: e.memset(VA[i][:, :, 64:65], ONESV), writes=[("VA", i)])


    vvb = new_vb("score")

    def vv_mm(e):
        e.matmul(PSB[vvb][0:16, 0:256], lhsT=T5S, rhs=OHAS, start=True, stop=True)
        ins = None
        for k in range(3):
            ins = e.matmul(PSB[vvb][0:16, 256:512], lhsT=RBS[:, k, :], rhs=OHBS[:, k, :], start=(k == 0), stop=(k == 2),
                           skip_group_check=True)
        return ins
    S.add("pe", vv_mm, reads=["t5s", "rbs", "ohas", "ohbs", "sq", ("x", 1)], writes=VR(vvb))
    S.add("dve", lambda e: e.tensor_copy(out=XS[1][0:16, 0:512], in_=PSB[vvb][0:16, :]),
          reads=VR(vvb), writes=["vvs"])
    dma("sp", vv_d.rearrange("l h n -> h l n"), VVS, ["vvs", ("x", 1)], ["vv_d"], "vv")
    for l in range(2):
        src1 = bass.AP(tensor=vv_d.tensor, offset=vv_d.offset + l * 16 * 256, ap=[[1, 128], [256, 16], [1, 64]])
        src2 = bass.AP(tensor=vv_d.tensor, offset=vv_d.offset + l * 16 * 256 + 128, ap=[[1, 64], [256, 16], [1, 64]])
        dma("pool", DN1[l][:], src1, ["vv_d"], [f"dn1_{l}"], "dn")
        dma("pool", DN2M[l * 64:(l + 1) * 64, :, :], src2, ["vv_d"], [f"dn2_{l}"], "dn")


    def load_w(Wt, wd, c0, ncols, dname, res, l_gain, after=()):
        for k in range(8):
            dma("pool", Wt[:, k, c0:c0 + ncols], wd[k * 128:(k + 1) * 128, c0:c0 + ncols], list(after), [(res, c0, k)], dname)
        if l_gain is not None:
            for k in range(8):
                c = C_GN[l_gain] + k
                S.add("dve", lambda e, k=k, c=c: e.tensor_scalar(out=Wt[:, k, c0:c0 + ncols], in0=Wt[:, k, c0:c0 + ncols],
                                                                 scalar1=SM[:, c:c + 1], scalar2=None, op0=ALU.mult),
                      reads=[(res, c0, k), (f"c_gn{l_gain}", k)], writes=[(res, c0, k)])
    load_w(WA, w_in_a, 1024, 256, "w0", "WA", 0)
    load_w(WA, w_in_a, 0, 1024, "w1", "WA", 0)
    load_w(WA, w_in_a, 1280, 1024, "w2", "WA", 0)

    def late_weights(stage, after=()):
        S.cur = []
        if stage == 0:
            load_w(WOA, w_out_a, 0, 1024, "w3", "WOA", None, after)
            load_w(WB, w_in_b, 1024, 1024, "w4", "WB", 1, after)
            load_w(WB, w_in_b, 2048, 1024, "w5", "WB", 1, after)
        else:
            load_w(WB, w_in_b, 0, 1024, "w6", "WB", 1, after)
            load_w(WB, w_in_b, 3072, 1024, "w7", "WB", 1, after)
            load_w(WOB, w_out_b, 0, 1024, "w8", "WOB", None, after)
        ops = S.cur
        S.cur = []
        return ops

    pa_ctr = [0]
    psb_ctr = [0]

    def next_ptr():
        return new_vb("tr")

    def next_pa():
        return new_vb("proj")

    def next_psb():
        return new_vb("score")

    def layer_cfg(l):
        if l == 0:
            return dict(W=WA, Wres="WA", WO=WOA, WOres="WOA", qoff=0, koff=1024, voff=1152, goff=1280, nkv=2)
        return dict(W=WB, Wres="WB", WO=WOB, WOres="WOB", qoff=0, koff=1024, voff=2048, goff=3072, nkv=16)

    def norm_and_transpose(xs, l):
        X = XS[xs]
        xr = ("x", xs)
        S.add("act", lambda e: e.activation(out=SQ[:], in_=X[:], func=AF.Square, scale=1.0 / 32.0,
                                            accum_out=SM[:, C_SS:C_SS + 1]),
              reads=[xr], writes=["sq", "c_ss"])
        S.add("dve", lambda e: e.tensor_scalar(out=SM[:, C_SS:C_SS + 1], in0=SM[:, C_SS:C_SS + 1], scalar1=EPS, scalar2=None,
                                               op0=ALU.add), reads=["c_ss"], writes=["c_ss"])
        S.add("pool", lambda e: e.tensor_tensor(out=SM[:, C_RX:C_RX + 1], in0=SM[:, C_SS:C_SS + 1],
                                                in1=SM[:, C_MHALF:C_MHALF + 1], op=ALU.pow),
              reads=["c_ss", "c_mhalf"], writes=["c_rx"])
        S.add("dve", lambda e: e.tensor_scalar(out=XN[:], in0=X[:], scalar1=SM[:, C_RX:C_RX + 1], scalar2=None, op0=ALU.mult),
              reads=[xr, "c_rx"], writes=["xn"])

        if DMA_T:
            for k in range(8):
                dma("sp", XT[:, k, :], XN[:, k * 128:(k + 1) * 128], ["xn"], [("xT", k)], "xt", transpose=True)
        else:
            def tr(e):
                ins = None
                for k in range(8):
                    ins = e.transpose(out=PTR[:, k * 128:(k + 1) * 128], in_=XN[:, k * 128:(k + 1) * 128], identity=IDENT[:])
                return ins
            S.add("pe", tr, reads=["xn", "ident"], writes=["ptr"])
            S.add("act", lambda e: e.activation(out=XT[:].rearrange("p k t -> p (k t)"), in_=PTR[:], func=AF.Copy),
                  reads=["ptr"], writes=[("xT", k) for k in range(8)])

    def proj_group(W, Wres, c0, ncols, pa):
        def mm(e):
            ins = None
            for k in range(8):
                for j in range(0, ncols, 512):
                    w = min(512, ncols - j)
                    ins = e.matmul(PA[pa][:, j:j + w], lhsT=XT[:, k, :], rhs=W[:, k, c0 + j:c0 + j + w],
                                   start=(k == 0), stop=(k == 7), skip_group_check=True)
            return ins
        S.add("pe", mm, reads=[("xT", k) for k in range(8)] + [(Wres, c0, k) for k in range(8)], writes=[("pa", pa)])

    def head_rstd(pa, c0, nh, l):
        w = nh * 64
        S.add("act", lambda e: e.activation(out=SQ[:, 0:w], in_=PA[pa][:, c0:c0 + w], func=AF.Square, scale=0.125),
              reads=[("pa", pa)], writes=["sq"])
        S.add("dve", lambda e: e.tensor_reduce(out=SM[:, C_SSQ:C_SSQ + nh], in_=SQ[:, 0:w].rearrange("p (h d) -> p h d", d=64),
                                               axis=AX.X, op=ALU.add), reads=["sq"], writes=["c_ssq"])
        S.add("dve", lambda e: e.tensor_scalar(out=SM[:, C_SSQ:C_SSQ + nh], in0=SM[:, C_SSQ:C_SSQ + nh], scalar1=EPS, scalar2=None,
                                               op0=ALU.add), reads=["c_ssq"], writes=["c_ssq"])
        S.add("pool", lambda e: e.tensor_tensor(out=SM[:, C_RQ:C_RQ + nh], in0=SM[:, C_SSQ:C_SSQ + nh],
                                                in1=SM[:, C_MHALF:C_MHALF + nh], op=ALU.pow),
              reads=["c_ssq", "c_mhalf"], writes=["c_rq"])

    def bc3(ap2d, nh, n):
        return bass.AP(tensor=ap2d.tensor, offset=ap2d.offset, ap=[list(ap2d.ap[0]), [1, nh], [0, n]])

    def kv_out_rows(kdst_fn, pa, c0, w, nh, l):
        S.add("dve", lambda e: e.tensor_tensor(out=SQ[:, 0:w].rearrange("p (h d) -> p h d", d=64),
                                               in0=PA[pa][:, c0:c0 + w].rearrange("p (h d) -> p h d", d=64),
                                               in1=bc3(SM[:, C_RQ:C_RQ + nh], nh, 64), op=ALU.mult),
              reads=[("pa", pa), "c_rq"], writes=["sq"])
        gk = GKBC[l]
        gkb = bass.AP(tensor=gk[:].tensor, offset=gk[:].offset, ap=[list(gk[:].ap[0]), [0, nh], [1, 64]])
        S.add("dve", lambda e: e.tensor_tensor(out=SQ[:, 0:w].rearrange("p (h d) -> p h d", d=64),
                                               in0=SQ[:, 0:w].rearrange("p (h d) -> p h d", d=64), in1=gkb, op=ALU.mult),
              reads=["sq", f"gkbc{l}"], writes=["sq"])
        kdst_fn(w)

    def v_out_rows(vdst_fn, pa, c0, w):
        S.add("act", lambda e: e.activation(out=SQ[:, 0:w], in_=PA[pa][:, c0:c0 + w], func=AF.Copy),
              reads=[("pa", pa)], writes=["sq"])
        vdst_fn(w)

    def sq_store(dsts):
        def f(w):
            for (d, p0, p1) in dsts:
                dma("sp", d, SQ[p0:p1, 0:w], ["sq"], [], "kv")
        return f

    def in_proj(l, xs, aslot, bslot, kdst=None, vdst=None, need_q=True, qs=0):
        QT, GA = QTS[qs], GAS[qs]
        cfg = layer_cfg(l)
        W, Wres = cfg["W"], cfg["Wres"]
        outer = S.cur
        pieces = {}

        def piece(name):
            S.cur = pieces.setdefault(name, [])
        piece("N")
        norm_and_transpose(xs, l)
        gqk = SM[:, C_GQK[l]:C_GQK[l] + 1]
        if need_q:
            pa = next_pa()
            piece("Pq")
            proj_group(W, Wres, 0, 1024, pa)
            piece("Cq")
            head_rstd(pa, 0, 16, l)
            S.add("dve", lambda e, pa=pa: e.tensor_tensor(out=QN[:].rearrange("p (h d) -> p h d", d=64),
                                                   in0=PA[pa][:, 0:1024].rearrange("p (h d) -> p h d", d=64),
                                                   in1=bc3(SM[:, C_RQ:C_RQ + 16], 16, 64), op=ALU.mult),
                  reads=[("pa", pa), "c_rq"], writes=["qn"])

            piece("Tq")

            def trq(e):
                ins = None
                for k in range(8):
                    ins = e.transpose(out=PTR[:, k * 128:(k + 1) * 128], in_=QN[:, k * 128:(k + 1) * 128], identity=IDENT[:])
                return ins
            S.add("pe", trq, reads=["qn", "ident"], writes=["ptr"])
            S.add("act", lambda e, pa=pa: e.mul(out=QT[0:64, :, 0, :], in_=PTR[0:64, :].rearrange("p (k t) -> p k t", t=128),
                                         mul=gqk[0:64, :]),
                  reads=["ptr", f"c_gqk{l}"], writes=qt_res(qs))
            S.add("dve", lambda e, pa=pa: e.tensor_scalar(out=QT[64:128, :, 1, :], in0=PTR[64:128, :].rearrange("p (k t) -> p k t", t=128),
                                                   scalar1=gqk[64:128, :], scalar2=None, op0=ALU.mult),
                  reads=["ptr", f"c_gqk{l}"], writes=qt_res(qs))
        if l == 0:
            pa = next_pa()
            piece("Pk")
            proj_group(W, Wres, 1024, 256, pa)
            piece("Ck")
            head_rstd(pa, 0, 2, l)
            kin = PA[pa][:, 0:128]
            kin_b = bass.AP(tensor=kin.tensor, offset=kin.offset, ap=[list(kin.ap[0]), [64, 2], [0, 2], [1, 64]])
            rq = SM[:, C_RQ:C_RQ + 2]
            rq_b = bass.AP(tensor=rq.tensor, offset=rq.offset, ap=[list(rq.ap[0]), [1, 2], [0, 2], [0, 64]])
            S.add("dve", lambda e, pa=pa: e.tensor_tensor(out=QN[:, 0:256].rearrange("p (g u d) -> p g u d", g=2, u=2), in0=kin_b, in1=rq_b,
                                                   op=ALU.mult), reads=[("pa", pa), "c_rq"], writes=["qn"])

            piece("Tk")

            def trk(e):
                ins = None
                for g in range(2):
                    ins = e.transpose(out=PTR[:, g * 128:(g + 1) * 128], in_=QN[:, g * 128:(g + 1) * 128], identity=IDENT[:])
                return ins
            S.add("pe", trk, reads=["qn", "ident"], writes=["ptr"])
            S.add("act", lambda e, pa=pa: e.activation(out=KTA[aslot][:].rearrange("p g t -> p (g t)"), in_=PTR[:, 0:256], func=AF.Copy),
                  reads=["ptr"], writes=[("KTA", aslot)])
            S.add("dve", lambda e, pa=pa: e.tensor_copy(out=VA[aslot][:, :, 0:64], in_=PA[pa][:, 128:256].rearrange("p (g d) -> p g d", d=64)),
                  reads=[("pa", pa)], writes=[("VA", aslot)])
            if kdst is not None:
                kv_out_rows(kdst, pa, 0, 128, 2, l)
                v_out_rows(vdst, pa, 128, 128)
        else:
            pa = next_pa()
            piece("Pk")
            proj_group(W, Wres, 1024, 1024, pa)
            piece("Ck")
            head_rstd(pa, 0, 16, l)
            S.add("dve", lambda e, pa=pa: e.tensor_tensor(out=QN[:].rearrange("p (h d) -> p h d", d=64),
                                                   in0=PA[pa][:, 0:1024].rearrange("p (h d) -> p h d", d=64),
                                                   in1=bc3(SM[:, C_RQ:C_RQ + 16], 16, 64), op=ALU.mult),
                  reads=[("pa", pa), "c_rq"], writes=["qn"])

            piece("Tk")

            def trk(e):
                ins = None
                for k in range(8):
                    ins = e.transpose(out=PTR[:, k * 128:(k + 1) * 128], in_=QN[:, k * 128:(k + 1) * 128], identity=IDENT[:])
                return ins
            S.add("pe", trk, reads=["qn", "ident"], writes=["ptr"])
            S.add("act", lambda e, pa=pa: e.activation(out=KTB[bslot][:].rearrange("p k t -> p (k t)"), in_=PTR[:], func=AF.Copy),
                  reads=["ptr"], writes=[("KTB", bslot)])
            if kdst is not None:
                kv_out_rows(kdst, pa, 0, 1024, 16, l)
            pa2 = next_pa()
            piece("Pv")
            proj_group(W, Wres, 2048, 1024, pa2)
            piece("Cv")
            S.add("dve", lambda e, pa2=pa2: e.tensor_copy(out=VB[bslot][:, :, 0:64], in_=PA[pa2][:, 0:1024].rearrange("p (h d) -> p h d", d=64)),
                  reads=[("pa", pa2)], writes=[("VB", bslot)])
            if vdst is not None:
                v_out_rows(vdst, pa2, 0, 1024)
        if need_q:
            pa = next_pa()
            piece("Pg")
            proj_group(W, Wres, cfg["goff"], 1024, pa)
            piece("Cg")
            S.add("act", lambda e, pa=pa: e.activation(out=GA[:], in_=PA[pa][:, 0:1024], func=AF.Tanh, scale=0.5),
                  reads=[("pa", pa)], writes=[("ga", qs, 0), ("ga", qs, 1)])
            S.add("dve", lambda e, pa=pa: e.scalar_tensor_tensor(out=GA[:], in0=GA[:], scalar=1.0, in1=PA[pa][:, 0:1024],
                                                          op0=ALU.add, op1=ALU.mult),
                  reads=[("ga", qs, 0), ("ga", qs, 1), ("pa", pa)], writes=[("ga", qs, 0), ("ga", qs, 1)])
        S.cur = outer
        for name in INPROJ_ORDER:
            S.cur.extend(pieces.get(name, []))
        assert set(pieces) <= set(INPROJ_ORDER), pieces.keys()

    PAR = VR
    PTR2 = VR
    PTR1 = lambda i, p: ("vb", i, p)

    xstat_ctr = [0]
    xbuf_ctr = [0]

    def sq_store2(dsts):
        def f(sq_off, w, dcol):
            for (d, p0, p1) in dsts:
                dma("sp", d[:, dcol:dcol + w], SQ[p0:p1, sq_off:sq_off + w], [("sq", sq_off // 512)], [], f"kv{sq_off // 512}")
        return f

    def in_proj2(l, xs, aslot, bslot, kdst=None, vdst=None, need_q=True, qs=0):
        QT, GA = QTS[qs], GAS[qs]
        cfg = layer_cfg(l)
        W, Wres = cfg["W"], cfg["Wres"]
        X = XS[xs]
        xr = ("x", xs)
        xb = xbuf_ctr[0] % NXB
        xbuf_ctr[0] += 1
        XN, XT = XNS[xb], XTS[xb]
        par = xstat_ctr[0] % 2
        xstat_ctr[0] += 1
        cSS, cRX, cE2, cRXH = C_XST + 4 * par, C_XST + 4 * par + 1, C_XST + 4 * par + 2, C_XST + 4 * par + 3
        rxs = ("c_rx", par)
        S.add("act", lambda e: e.activation(out=XN[:], in_=X[:], func=AF.Square, scale=1.0 / 32.0, accum_out=SM[:, cSS:cSS + 1]),
              reads=[xr], writes=[("xn", xb), ("c_ss", par)])
        S.add("dve", lambda e: e.tensor_copy(out=XN[:], in_=X[:]), reads=[xr], writes=[("xn", xb)])
        S.add("dve", lambda e: e.tensor_scalar(out=SM[:, cSS:cSS + 1], in0=SM[:, cSS:cSS + 1], scalar1=EPS, scalar2=None,
                                               op0=ALU.add), reads=[("c_ss", par)], writes=[("c_ss", par)])
        S.add("dve", lambda e: e.tensor_scalar(out=SM[:, cE2:cE2 + 1], in0=SM[:, cSS:cSS + 1], scalar1=EPS, scalar2=None,
                                               op0=ALU.mult), reads=[("c_ss", par)], writes=[("c_e2", par)])
        S.add("pool", lambda e: e.tensor_tensor(out=SM[:, cRX:cRX + 1], in0=SM[:, cSS:cSS + 1],
                                                in1=SM[:, C_MHALF:C_MHALF + 1], op=ALU.pow),
              reads=[("c_ss", par), "c_mhalf"], writes=[rxs])
        S.add("dve", lambda e: e.tensor_scalar(out=SM[:, cRXH:cRXH + 1], in0=SM[:, cRX:cRX + 1], scalar1=0.5, scalar2=None,
                                               op0=ALU.mult), reads=[rxs], writes=[("c_rxh", par)])
        RX = SM[:, cRX:cRX + 1]
        RXH = SM[:, cRXH:cRXH + 1]
        E2 = SM[:, cE2:cE2 + 1]
        pt = next_ptr()

        def trx(e, pt=pt):
            ins = None
            for k in range(8):
                ins = e.transpose(out=PTRS[pt][:, k * 128:(k + 1) * 128], in_=XN[:, k * 128:(k + 1) * 128], identity=IDENT[:])
            return ins
        S.add("pe", trx, reads=[("xn", xb), "ident"], writes=[*PTR2(pt)])
        S.add("act", lambda e, pt=pt: e.activation(out=XT[:].rearrange("p k t -> p (k t)"), in_=PTRS[pt][:], func=AF.Copy),
              reads=[*PTR2(pt)], writes=[("xT", xb, k) for k in range(8)])
        gqk = SM[:, C_GQK[l]:C_GQK[l] + 1]

        def proj(c0, ncols, wkey):
            pa = next_pa()

            def mm(e):
                ins = None
                for k in range(8):
                    ins = e.matmul(PA[pa][:, 0:ncols], lhsT=XT[:, k, :], rhs=W[:, k, c0:c0 + ncols],
                                   start=(k == 0), stop=(k == 7), skip_group_check=True)
                return ins
            S.add("pe", mm, reads=[("xT", xb, k) for k in range(8)] + [(Wres, wkey, k) for k in range(8)], writes=[*PAR(pa)])
            return pa

        def rstd_chain(pa, pcol, nh, sq_off, ccol, cres):
            w = nh * 64
            hs = sq_off // 512
            S.add("act", lambda e: e.activation(out=SQ[:, sq_off:sq_off + w], in_=PA[pa][:, pcol:pcol + w], func=AF.Square, scale=0.125),
                  reads=[*PAR(pa)], writes=[("sq", hs)])
            S.add("dve", lambda e: e.tensor_reduce(out=SM[:, C_SSQ + ccol:C_SSQ + ccol + nh],
                                                   in_=SQ[:, sq_off:sq_off + w].rearrange("p (h d) -> p h d", d=64),
                                                   axis=AX.X, op=ALU.add), reads=[("sq", hs)], writes=[("c_ssq", cres)])
            S.add("dve", lambda e: e.tensor_scalar(out=SM[:, C_SSQ + ccol:C_SSQ + ccol + nh], in0=SM[:, C_SSQ + ccol:C_SSQ + ccol + nh],
                                                   scalar1=E2, scalar2=None, op0=ALU.add), reads=[("c_ssq", cres), ("c_e2", par)], writes=[("c_ssq", cres)])
            S.add("pool", lambda e: e.tensor_tensor(out=SM[:, C_RQ + ccol:C_RQ + ccol + nh], in0=SM[:, C_SSQ + ccol:C_SSQ + ccol + nh],
                                                    in1=SM[:, C_MHALF:C_MHALF + nh], op=ALU.pow),
                  reads=[("c_ssq", cres), "c_mhalf"], writes=[("c_rq", cres)])

        def k_out(pa, pcol, nh, sq_off, ccol, cres, dcol):
            w = nh * 64
            hs = sq_off // 512
            sqv = SQ[:, sq_off:sq_off + w].rearrange("p (h d) -> p h d", d=64)
            S.add("dve", lambda e: e.tensor_tensor(out=sqv, in0=PA[pa][:, pcol:pcol + w].rearrange("p (h d) -> p h d", d=64),
                                                   in1=bc3(SM[:, C_RQ + ccol:C_RQ + ccol + nh], nh, 64), op=ALU.mult),
                  reads=[*PAR(pa), ("c_rq", cres)], writes=[("sq", hs)])
            gk = GKBC[l]
            gkb = bass.AP(tensor=gk[:].tensor, offset=gk[:].offset, ap=[list(gk[:].ap[0]), [0, nh], [1, 64]])
            S.add("dve", lambda e: e.tensor_tensor(out=sqv, in0=sqv, in1=gkb, op=ALU.mult),
                  reads=[("sq", hs), f"gkbc{l}"], writes=[("sq", hs)])
            kdst(sq_off, w, dcol)

        def v_out(pa, pcol, w, sq_off, dcol):
            hs = sq_off // 512
            S.add("act", lambda e: e.mul(out=SQ[:, sq_off:sq_off + w], in_=PA[pa][:, pcol:pcol + w], mul=RX),
                  reads=[*PAR(pa), rxs], writes=[("sq", hs)])
            vdst(sq_off, w, dcol)

        def norm_T(pa, h, cres, dst_kind):
            ccol = 8 * cres
            S.add("dve", lambda e: e.tensor_tensor(out=QN[:, h * 512:(h + 1) * 512].rearrange("p (h d) -> p h d", d=64),
                                                   in0=PA[pa][:, 0:512].rearrange("p (h d) -> p h d", d=64),
                                                   in1=bc3(SM[:, C_RQ + ccol:C_RQ + ccol + 8], 8, 64), op=ALU.mult),
                  reads=[*PAR(pa), ("c_rq", cres)], writes=[("qn", h)])
            pt = next_ptr()

            def tr(e):
                ins = None
                for j in range(4):
                    ins = e.transpose(out=PTRS[pt][:, j * 128:(j + 1) * 128], in_=QN[:, h * 512 + j * 128:h * 512 + (j + 1) * 128],
                                      identity=IDENT[:])
                return ins
            S.add("pe", tr, reads=[("qn", h), "ident"], writes=[*PTR2(pt)])
            if dst_kind == "q":
                S.add("act", lambda e: e.mul(out=QT[0:64, 4 * h:4 * h + 4, 0, :],
                                             in_=PTRS[pt][0:64, 0:512].rearrange("p (k t) -> p k t", t=128), mul=gqk[0:64, :]),
                      reads=[PTR1(pt, 0), f"c_gqk{l}"], writes=[("qT", qs, h, 0)])
                S.add("dve", lambda e: e.tensor_scalar(out=QT[64:128, 4 * h:4 * h + 4, 1, :],
                                                       in0=PTRS[pt][64:128, 0:512].rearrange("p (k t) -> p k t", t=128),
                                                       scalar1=gqk[64:128, :], scalar2=None, op0=ALU.mult),
                      reads=[PTR1(pt, 1), f"c_gqk{l}"], writes=[("qT", qs, h, 1)])
            else:
                S.add("act", lambda e: e.activation(out=KTB[bslot][:, 4 * h:4 * h + 4, :].rearrange("p k t -> p (k t)"),
                                                    in_=PTRS[pt][:, 0:512], func=AF.Copy),
                      reads=[*PTR2(pt)], writes=[("KTB", bslot)])

        if need_q:
            for h in range(2):
                pa = proj(h * 512, 512, 0)
                rstd_chain(pa, 0, 8, h * 512, 8 * h, h)
                norm_T(pa, h, h, "q")
        if l == 0:
            pa = proj(1024, 256, 1024)
            rstd_chain(pa, 0, 2, 0, 16, 2)
            def knd(e, pa=pa):
                kin = PA[pa][:, 0:128]
                kin_b = bass.AP(tensor=kin.tensor, offset=kin.offset, ap=[list(kin.ap[0]), [64, 2], [0, 2], [1, 64]])
                rq = SM[:, C_RQ + 16:C_RQ + 18]
                rq_b = bass.AP(tensor=rq.tensor, offset=rq.offset, ap=[list(rq.ap[0]), [1, 2], [0, 2], [0, 64]])
                return e.tensor_tensor(out=QN[:, 0:256].rearrange("p (g u d) -> p g u d", g=2, u=2), in0=kin_b, in1=rq_b, op=ALU.mult)
            S.add("dve", knd, reads=[*PAR(pa), ("c_rq", 2)], writes=[("qn", 0)])
            pt = next_ptr()

            def trk(e):
                ins = None
                for g in range(2):
                    ins = e.transpose(out=PTRS[pt][:, g * 128:(g + 1) * 128], in_=QN[:, g * 128:(g + 1) * 128], identity=IDENT[:])
                return ins
            S.add("pe", trk, reads=[("qn", 0), "ident"], writes=[*PTR2(pt)])
            S.add("act", lambda e: e.activation(out=KTA[aslot][:].rearrange("p g t -> p (g t)"), in_=PTRS[pt][:, 0:256], func=AF.Copy),
                  reads=[*PTR2(pt)], writes=[("KTA", aslot)])
            S.add("dve", lambda e, pa=pa: e.tensor_scalar(out=VA[aslot][:, :, 0:64], in0=PA[pa][:, 128:256].rearrange("p (g d) -> p g d", d=64),
                                                              scalar1=RX, scalar2=None, op0=ALU.mult),
                  reads=[*PAR(pa), rxs], writes=[("VA", aslot)])
            if kdst is not None:
                k_out(pa, 0, 2, 0, 16, 2, 0)
                v_out(pa, 128, 128, 512, 0)
        else:
            for h in range(2):
                pa = proj(1024 + h * 512, 512, 1024)
                rstd_chain(pa, 0, 8, h * 512, 16 + 8 * h, 2 + h)
                norm_T(pa, h, 2 + h, "k")
                if kdst is not None:
                    k_out(pa, 0, 8, h * 512, 16 + 8 * h, 2 + h, h * 512)
            for h in range(2):
                pa = proj(2048 + h * 512, 512, 2048)
                S.add("dve", lambda e, pa=pa, h=h: e.tensor_scalar(out=VB[bslot][:, 8 * h:8 * h + 8, 0:64],
                                                                   in0=PA[pa][:, 0:512].rearrange("p (h d) -> p h d", d=64),
                                                                   scalar1=RX, scalar2=None, op0=ALU.mult),
                      reads=[*PAR(pa), rxs], writes=[("VB", bslot)])
                if vdst is not None:
                    v_out(pa, 0, 512, h * 512, h * 512)
        if need_q:
            for h in range(2):
                pa = proj(cfg["goff"] + h * 512, 512, cfg["goff"])
                S.add("act", lambda e, pa=pa, h=h: e.activation(out=GA[:, h * 512:(h + 1) * 512], in_=PA[pa][:, 0:512], func=AF.Tanh, scale=RXH),
                      reads=[*PAR(pa), ("c_rxh", par)], writes=[("ga", qs, h)])
                S.add("dve", lambda e, pa=pa, h=h: e.scalar_tensor_tensor(out=GA[:, h * 512:(h + 1) * 512], in0=GA[:, h * 512:(h + 1) * 512],
                                                                          scalar=1.0, in1=PA[pa][:, 0:512], op0=ALU.add, op1=ALU.mult),
                      reads=[("ga", qs, h), *PAR(pa)], writes=[("ga", qs, h)])
                S.add("dve", lambda e, h=h: e.tensor_scalar(out=GA[:, h * 512:(h + 1) * 512], in0=GA[:, h * 512:(h + 1) * 512],
                                                            scalar1=RX, scalar2=None, op0=ALU.mult),
                      reads=[("ga", qs, h), rxs], writes=[("ga", qs, h)])

    BIAS_PREV = [(191, 128, 1, 0), (63, 64, 2, 0), (127, 64, 2, 1)]
    BIAS_CUR = [(63, 128, 1, 0), (127, 128, 1, 1)]

    def BIAS_CACHE(s):
        return [(191, 128, 1, s), (63, 64, 2, s)]

    def attention(l, key_tiles, rows, qs=0):
        QT = QTS[qs]
        r0, r1 = rows
        nkt = len(key_tiles)

        def mk_qk(hg, ti, kt, b):
            L2 = len(kt["bias"]) > 0

            def qk(e):
                ins = None
                nb = len(kt["bias"])
                first = True
                for pl in range(2):
                    pair = hg * 2 + pl
                    if not L2:
                        ins = e.matmul(PSB[b][:, pl * 256:(pl + 1) * 256], lhsT=kt["kt"](pair),
                                       rhs=QT[:, pair, :, :].rearrange("p a t -> p (a t)"),
                                       start=first, stop=(pl == 1), skip_group_check=True)
                        first = False
                    else:
                        for par in range(2):
                            ins = e.matmul(PSB[b][:, par * 256 + pl * 128:par * 256 + (pl + 1) * 128], lhsT=kt["kt"](pair),
                                           rhs=QT[:, pair, :, par * 64:(par + 1) * 64],
                                           start=first, stop=False, skip_group_check=True)
                            first = False
                for bi, (c, K, dn, par) in enumerate(kt["bias"]):
                    if dn == 1:
                        lt, rh = ADM[0:K, 255 - c:255 - c + 128], DN1[l][0:K, hg * 4:(hg + 1) * 4, :]
                    else:
                        lt = ADM[l * 64:l * 64 + 64, 255 - c - 64 * l:255 - c - 64 * l + 128]
                        rh = DN2M[l * 64:l * 64 + 64, hg * 4:(hg + 1) * 4, :]
                    ins = e.matmul(PSB[b][:, par * 256:(par + 1) * 256], lhsT=lt, rhs=rh.rearrange("p h t -> p (h t)"),
                                   start=False, stop=(bi == nb - 1), skip_group_check=True)
                return ins
            return qk

        def add_exp(ti, kt, b):
            L2 = len(kt["bias"]) > 0
            em = kt["em"]
            if em[0] == em[1]:
                if L2:
                    o_ap = PT[:, ti, :].rearrange("p (h par t) -> p par h t", h=4, par=2)
                    i_ap = PSB[b][:, :].rearrange("p (par h t) -> p par h t", par=2, h=4)
                else:
                    o_ap, i_ap = PT[:, ti, :], PSB[b][:, :]
                def ex(e):
                    if L2:
                        o_ap = PT[:, ti, :].rearrange("p (h par t) -> p par h t", h=4, par=2)
                        i_ap = PSB[b][:, :].rearrange("p (par h t) -> p par h t", par=2, h=4)
                    else:
                        o_ap, i_ap = PT[:, ti, :], PSB[b][:, :]
                    return e.activation(out=o_ap, in_=i_ap, func=AF.Exp, bias=SM[:, em[0]:em[0] + 1])
                S.add("act", ex, reads=[*VR(b), "c_mask", "c_zero", "c_lohi"], writes=[("PT", ti)])
            else:
                for par in range(2):
                    o_ap = PT[:, ti, :].rearrange("p (h t) -> p h t", t=128)[:, :, par * 64:(par + 1) * 64]
                    if L2:
                        i_ap = PSB[b][:, par * 256:(par + 1) * 256].rearrange("p (h t) -> p h t", t=64)
                    else:
                        i_ap = PSB[b][:, :].rearrange("p (h t) -> p h t", t=128)[:, :, par * 64:(par + 1) * 64]
                    def ex2(e, par=par):
                        o_ap = PT[:, ti, :].rearrange("p (h t) -> p h t", t=128)[:, :, par * 64:(par + 1) * 64]
                        if L2:
                            i_ap = PSB[b][:, par * 256:(par + 1) * 256].rearrange("p (h t) -> p h t", t=64)
                        else:
                            i_ap = PSB[b][:, :].rearrange("p (h t) -> p h t", t=128)[:, :, par * 64:(par + 1) * 64]
                        return e.activation(out=o_ap, in_=i_ap, func=AF.Exp, bias=SM[:, em[par]:em[par] + 1])
                    S.add("act", ex2, reads=[*VR(b), "c_mask", "c_zero", "c_lohi"], writes=[("PT", ti)])

        def pt_lhsT(kt, ti, hl, q0, q1, k0, k1):
            return PT[k0:k1, ti, hl * 128 + q0:hl * 128 + q1]

        GA = GAS[qs]
        tblocks = []
        for ti, kt in enumerate(key_tiles):
            bl = []
            for (q0, q1, k0, k1) in kt["blocks"]:
                q0c, q1c = max(q0, r0), min(q1, r1)
                if q0c < q1c:
                    bl.append((q0c, q1c, k0, k1))
            tblocks.append(bl)
        assert all(len(bl) == 1 and bl[0][0] == r0 and bl[0][1] == r1 for bl in tblocks), "one block per tile covering all rows"

        po_of = {}

        def POV(hg):
            if hg not in po_of:
                po_of[hg] = new_vb("po")
            return po_of[hg]

        def add_pv_tile(hg, ti):
            kt = key_tiles[ti]
            (q0, q1, k0, k1) = tblocks[ti][0]
            pov = POV(hg)

            def pv(e):
                ins = None
                for hl in range(4):
                    head = hg * 4 + hl
                    ins = e.matmul(PA[pov][q0:q1, hl * 65:(hl + 1) * 65], lhsT=pt_lhsT(kt, ti, hl, q0, q1, k0, k1),
                                   rhs=kt["v"](head, k0, k1), start=(ti == 0 and hl == 0), stop=(ti == nkt - 1),
                                   skip_group_check=True)
                return ins
            S.add("pe", pv, reads=[("PT", ti), kt["vres"]], writes=VR(pov))

        def add_norm(hg):
            pov = POV(hg)
            den = lambda: PA[pov][r0:r1, 0:260].rearrange("p (h c) -> p h c", c=65)[:, :, 64:65]
            oin = lambda: PA[pov][r0:r1, 0:260].rearrange("p (h c) -> p h c", c=65)[:, :, 0:64]
            if l == 0:
                S.add("dve", lambda e: e.tensor_tensor(out=SM[r0:r1, C_DEN:C_DEN + 4].rearrange("p (h c) -> p h c", c=1), in0=den(),
                                                       in1=ESBC[r0:r1, hg * 4:(hg + 1) * 4].rearrange("p (h c) -> p h c", c=1),
                                                       op=ALU.add), reads=[*VR(pov), "esbc"], writes=["c_den"])
                S.add("dve", lambda e: e.reciprocal(out=SM[r0:r1, C_RDEN:C_RDEN + 4], in_=SM[r0:r1, C_DEN:C_DEN + 4]),
                      reads=["c_den"], writes=["c_rden"])
            else:
                S.add("dve", lambda e: e.reciprocal(out=SM[r0:r1, C_RDEN:C_RDEN + 4].rearrange("p (h c) -> p h c", c=1), in_=den()),
                      reads=VR(pov), writes=["c_rden"])
            S.add("dve", lambda e: e.tensor_tensor(out=OB[r0:r1, hg * 256:(hg + 1) * 256].rearrange("p (h d) -> p h d", d=64),
                                                   in0=oin(), in1=bc3(SM[r0:r1, C_RDEN:C_RDEN + 4], 4, 64), op=ALU.mult),
                  reads=[*VR(pov), "c_rden"], writes=[("ob", hg)])
            S.add("dve", lambda e: e.tensor_tensor(out=OB[r0:r1, hg * 256:(hg + 1) * 256], in0=OB[r0:r1, hg * 256:(hg + 1) * 256],
                                                   in1=GA[r0:r1, hg * 256:(hg + 1) * 256], op=ALU.mult),
                  reads=[("ob", hg), ("ga", qs, hg // 2)], writes=[("ob", hg)])

        early = min(2, nkt)
        pre = {}
        qk_reads = lambda kt: [kt["kres"], "adm", f"dn1_{l}", f"dn2_{l}"] + qt_res(qs)
        for hg in range(4):
            for ti, kt in enumerate(key_tiles):
                if (hg, ti) in pre:
                    b = pre[(hg, ti)]
                else:
                    b = next_psb()
                    S.add("pe", mk_qk(hg, ti, kt, b), reads=qk_reads(kt), writes=VR(b))
                add_exp(ti, kt, b)
                if ti >= 1:
                    add_pv_tile(hg, ti - 1)
            if hg < 3:
                for ti in range(early):
                    b = next_psb()
                    pre[(hg + 1, ti)] = b
                    S.add("pe", mk_qk(hg + 1, ti, key_tiles[ti], b), reads=qk_reads(key_tiles[ti]), writes=VR(b))
            add_pv_tile(hg, nkt - 1)
            add_norm(hg)

    def out_proj_residual(l, xs, qs=0):
        cfg = layer_cfg(l)
        WO, WOres = cfg["WO"], cfg["WOres"]
        vt = new_vb("tr")

        def tr(e):
            ins = None
            for k in range(8):
                ins = e.transpose(out=PTRS[vt][:, k * 128:(k + 1) * 128], in_=OB[:, k * 128:(k + 1) * 128], identity=IDENT[:])
            return ins
        S.add("pe", tr, reads=[("ob", h_) for h_ in range(4)] + ["ident"], writes=VR(vt))
        S.add("act", lambda e: e.activation(out=OGT[:].rearrange("p k t -> p (k t)"), in_=PTRS[vt][:], func=AF.Copy),
              reads=VR(vt), writes=[("ogT", k) for k in range(8)])
        X = XS[xs]
        for j in range(2):
            vy = new_vb("y")

            def mm(e, j=j, vy=vy):
                ins = None
                for k in range(8):
                    ins = e.matmul(PA[vy][:, 0:512], lhsT=OGT[:, k, :], rhs=WO[:, k, j * 512:(j + 1) * 512],
                                   start=(k == 0), stop=(k == 7), skip_group_check=True)
                return ins
            S.add("pe", mm, reads=[("ogT", k) for k in range(8)] + [(WOres, 0, k) for k in range(8)], writes=VR(vy))
            S.add("dve", lambda e, j=j, vy=vy: e.tensor_tensor(out=X[:, j * 512:(j + 1) * 512], in0=X[:, j * 512:(j + 1) * 512],
                                                               in1=PA[vy][:, 0:512], op=ALU.add),
                  reads=[("x", xs), *VR(vy)], writes=[("x", xs)])

    def emcols(mask, edge):
        base = C_MASK if mask else C_ZERO
        if edge == "prev":
            return (base, C_LOH if mask else C_LO)
        if edge == "cur":
            return (C_HIH if mask else C_HI, base)
        return (base, base)

    def ktA(slot, bias, blocks, mask, edge=None):
        return dict(kt=lambda pair, slot=slot: KTA[slot][:, pair // 4, :], kres=("KTA", slot),
                    v=lambda head, k0, k1, slot=slot: VA[slot][k0:k1, head // 8, :], vres=("VA", slot),
                    bias=bias, blocks=blocks, em=emcols(mask, edge))

    def ktB(slot, bias, blocks, mask, edge=None):
        return dict(kt=lambda pair, slot=slot: KTB[slot][:, pair, :], kres=("KTB", slot),
                    v=lambda head, k0, k1, slot=slot: VB[slot][k0:k1, head, :], vres=("VB", slot),
                    bias=bias, blocks=blocks, em=emcols(mask, edge))

    setup_ops = S.cur
    S.cur = []
    qs_of = {}
    qctr = [0]

    def new_qs(key):
        qs_of[key] = qctr[0] % NQS
        qctr[0] += 1
        return qs_of[key]

    IP = in_proj2 if HALF else in_proj
    SQS = sq_store2 if HALF else sq_store

    def gen_phase(gi, ph):
        S.cur = []
        sample = (gi == NU)
        xs = gi % 2
        is_halo = gi < NH
        m = gi - NH
        if not sample:
            if ph == "A12":
                kdst = vdst = None
                if gi == NU - 1:
                    kdst = SQS([(kap[:, :], 0, 128)])
                    vdst = SQS([(vap[:, :], 0, 128)])
                IP(0, xs, gi % 3, None, kdst, vdst, need_q=(gi >= 1), qs=new_qs((gi, 0)) if gi >= 1 else 0)
            elif ph == "A34":
                if gi >= 1:
                    tiles = [ktA((gi - 1) % 3, BIAS_PREV, [(0, 128, 0, 128)], (gi - 1) < NH, "prev"),
                             ktA(gi % 3, BIAS_CUR, [(0, 128, 0, 128)], is_halo, "cur")]
                    attention(0, tiles, (0, 128), qs=qs_of[(gi, 0)])
                    out_proj_residual(0, xs, qs=qs_of[(gi, 0)])
            elif ph == "B12":
                if gi >= 1:
                    kdst = vdst = None
                    if (not is_halo) and m >= nm - 4:
                        j = m - (nm - 4)
                        kdst = SQS([(kbp[j * 128:(j + 1) * 128, :], 0, 128)])
                        vdst = SQS([(vbp[j * 128:(j + 1) * 128, :], 0, 128)])
                    IP(1, xs, None, gi % NKB, kdst, vdst, need_q=(not is_halo),
                            qs=new_qs((gi, 1)) if not is_halo else 0)
            elif ph == "B34":
                if not is_halo:
                    tiles = []
                    for t in range(gi - 4, gi + 1):
                        blocks, edge = [(0, 128, 0, 128)], None
                        if t == gi - 4:
                            bias, edge = [], "prev"
                        elif t == gi - 1:
                            bias = BIAS_PREV
                        elif t == gi:
                            bias, edge = BIAS_CUR, "cur"
                        else:
                            bias = []
                        tiles.append(ktB(t % NKB, bias, blocks, t < NH, edge))
                    attention(1, tiles, (0, 128), qs=qs_of[(gi, 1)])
                    out_proj_residual(1, xs, qs=qs_of[(gi, 1)])
                    dma("sp", y_p[m * 128:(m + 1) * 128, :], XS[xs][:], [("x", xs)], [], f"y{xs}")
        else:
            OGF = OGT[:].rearrange("p k t -> p (k t)")
            own_a = NU % 3
            ca = [(NU + 1) % 3, (NU + 2) % 3]
            own_b = NU % NKB
            cb = [q for q in range(NKB) if q != own_b][:4]
            if ph == "A12":
                kdst = SQS([(kas[0, 64:128, :], 0, 64), (kas[1, 64:128, :], 64, 128)])
                vdst = SQS([(vas[0, 64:128, :], 0, 64), (vas[1, 64:128, :], 64, 128)])
                IP(0, xs, own_a, None, kdst, vdst, need_q=True, qs=new_qs((gi, 0)))
            elif ph == "A34":
                for s_ in range(2):
                    for u in range(2):
                        dst = OGF[:, 0:256].rearrange("p (g u d) -> p g u d", g=2, u=2)[:, :, u, :]
                        dma("pool", dst, cka[s_, :, :].rearrange("p (g d) -> p g d", d=64), [], [("ogT", k_) for k_ in range(8)], "cache")

                    vc = new_vb("tr")

                    def trc(e, vc=vc):
                        ins = None
                        for g in range(2):
                            ins = e.transpose(out=PTRS[vc][:, g * 128:(g + 1) * 128], in_=OGF[:, g * 128:(g + 1) * 128], identity=IDENT[:])
                        return ins
                    S.add("pe", trc, reads=[("ogT", k_) for k_ in range(8)] + ["ident"], writes=VR(vc))
                    S.add("act", lambda e, s_=s_, vc=vc: e.activation(out=KTA[ca[s_]][:].rearrange("p g t -> p (g t)"), in_=PTRS[vc][:, 0:256], func=AF.Copy),
                          reads=VR(vc), writes=[("KTA", ca[s_])])
                    dma("pool", VA[ca[s_]][:, :, 0:64], cva[s_, :, :].rearrange("p (g d) -> p g d", d=64), [], [("VA", ca[s_])], f"cva{s_}")
                for s_ in range(2):
                    rows = (s_ * 64, (s_ + 1) * 64)
                    tiles = [ktA(ca[s_], BIAS_CACHE(s_), [(rows[0], rows[1], 0, 128)], False),
                             ktA(own_a, BIAS_CUR, [(rows[0], rows[1], rows[0], rows[1])], False)]
                    attention(0, tiles, rows, qs=qs_of[(gi, 0)])
                out_proj_residual(0, xs, qs=qs_of[(gi, 0)])
            elif ph == "B12":
                kdst = SQS([(kbs[0, 448:512, :], 0, 64), (kbs[1, 448:512, :], 64, 128)])
                vdst = SQS([(vbs[0, 448:512, :], 0, 64), (vbs[1, 448:512, :], 64, 128)])
                IP(1, xs, None, own_b, kdst, vdst, need_q=True, qs=new_qs((gi, 1)))
            elif ph == "B34":
                for s_ in range(2):
                    rows = (s_ * 64, (s_ + 1) * 64)
                    for t in range(4):
                        dma("pool", OGF, ckb[s_, t * 128:(t + 1) * 128, :], [], [("ogT", k_) for k_ in range(8)], "cache")

                        vc = new_vb("tr")

                        def trc(e, vc=vc):
                            ins = None
                            for k in range(8):
                                ins = e.transpose(out=PTRS[vc][:, k * 128:(k + 1) * 128], in_=OGF[:, k * 128:(k + 1) * 128], identity=IDENT[:])
                            return ins
                        S.add("pe", trc, reads=[("ogT", k_) for k_ in range(8)] + ["ident"], writes=VR(vc))
                        S.add("act", lambda e, t=t, vc=vc: e.activation(out=KTB[cb[t]][:].rearrange("p k t -> p (k t)"), in_=PTRS[vc][:], func=AF.Copy),
                              reads=VR(vc), writes=[("KTB", cb[t])])
                        dma("pool", VB[cb[t]][:, :, 0:64], cvb[s_, t * 128:(t + 1) * 128, :].rearrange("p (h d) -> p h d", d=64), [], [("VB", cb[t])], f"cv{t}")
                    tiles = []
                    for t in range(4):
                        tiles.append(ktB(cb[t], BIAS_CACHE(s_) if t == 3 else [], [(rows[0], rows[1], 0, 128)], False))
                    tiles.append(ktB(own_b, BIAS_CUR, [(rows[0], rows[1], rows[0], rows[1])], False))
                    attention(1, tiles, rows, qs=qs_of[(gi, 1)])
                out_proj_residual(1, xs, qs=qs_of[(gi, 1)])
                dma("sp", y_s[:, :], XS[xs][:], [("x", xs)], [], f"y{xs}")
        ops = S.cur
        S.cur = []
        return ops

    def gen_xload(g):
        S.cur = []
        if g is not None:
            if g < NU:
                dma("sp", XS[g % 2][:], xin[g * 128:(g + 1) * 128, :], [], [("x", g % 2)], f"x{g % 2}")
            else:
                dma("sp", XS[g % 2][:], xs_in[:, :], [], [("x", g % 2)], f"x{g % 2}")
        ops = S.cur
        S.cur = []
        return ops

    TL = _Timeline()
    tl_end = [0.0]

    def tl_commit_list(lst):
        for rec in lst:
            tl_end[0] = max(tl_end[0], TL.commit(rec))
        return lst

    def merge(a, b):
        out = []
        i = j = 0
        na, nb = len(a), len(b)
        lane_end = [0.0, 0.0]
        t_begin = tl_end[0]
        while i < na or j < nb:
            if j >= nb:
                pick = 0
            elif i >= na:
                pick = 1
            elif GREEDY:
                ta, tb = TL.start_time(a[i]), TL.start_time(b[j])
                if abs(ta - tb) < 1.0:
                    pick = 0 if i * nb <= j * na else 1
                else:
                    pick = 0 if ta < tb else 1
            else:
                pick = 0 if i * nb <= j * na else 1
            rec = a[i] if pick == 0 else b[j]
            if pick == 0:
                i += 1
            else:
                j += 1
            fin = TL.commit(rec)
            lane_end[pick] = max(lane_end[pick], fin)
            tl_end[0] = max(tl_end[0], fin)
            out.append(rec)
        if _os0.environ.get("KTLV"):
            pe_a = sum(TL._cost(r)[0] for r in a if r[0] == "pe") / 1e3
            pe_b = sum(TL._cost(r)[0] for r in b if r[0] == "pe") / 1e3
            print("SLOT t0=%.1f lane12: n=%d end=%.1f pe=%.1f | lane34: n=%d end=%.1f pe=%.1f" % (t_begin / 1e3, na, lane_end[0] / 1e3, pe_a, nb, lane_end[1] / 1e3, pe_b))
        return out

    units = list(range(NU)) + ([NU] if do_sample else [])

    def gen_cpy(after=()):
        S.cur = []
        if do_sample:
            for s_ in range(2):
                dma("sp", kas[s_, 0:64, :], cka[s_, 64:128, :], list(after), [], "cpy")
                dma("sp", vas[s_, 0:64, :], cva[s_, 64:128, :], list(after), [], "cpy")
                dma("sp", kbs[s_, 0:448, :], ckb[s_, 64:512, :], list(after), [], "cpy")
                dma("sp", vbs[s_, 0:448, :], cvb[s_, 64:512, :], list(after), [], "cpy")
        ops = S.cur
        S.cur = []
        return ops

    if DAGSCHED:
        sem = list(setup_ops)
        cpy_unit = min(10, NU - 1)
        for i_u, u in enumerate(units):
            sem += gen_xload(u)
            for ph in ("A12", "A34", "B12", "B34"):
                sem += gen_phase(u, ph)
                if i_u == 0 and ph == "A12":
                    sem += late_weights(0, after=[("KTA", u % 3)])
                if i_u == 1 and ph == "B12":
                    sem += late_weights(1, after=[("KTB", u % NKB)])
                if u == cpy_unit and ph == "B12":
                    sem += gen_cpy(after=[("KTB", u % NKB)])
        n_ops = len(sem)
        _free = set(_os0.environ.get("KFREE", "").split(",")) - {""}
        if _free:
            ver = {}
            ren = []
            base = lambda r: (r[0] if isinstance(r, tuple) else r)
            for (eng, fn, reads, writes, dsem) in sem:
                rd = [((r, ver.get(r, 0)) if base(r) in _free else r) for r in reads]
                wr = []
                for w in writes:
                    if base(w) in _free:
                        if w not in reads:
                            ver[w] = ver.get(w, 0) + 1
                        wr.append((w, ver.get(w, 0)))
                    else:
                        wr.append(w)
                ren.append((eng, fn, rd, wr, dsem))
            sem = ren
        lastw, readers = {}, {}
        preds = [set() for _ in range(n_ops)]
        for i, (eng, fn, reads, writes, dsem) in enumerate(sem):
            for r in reads:
                if r in lastw:
                    preds[i].add(lastw[r])
            for w in writes:
                if w in lastw:
                    preds[i].add(lastw[w])
                for q in readers.get(w, ()):
                    preds[i].add(q)
            preds[i].discard(i)
            for r in reads:
                readers.setdefault(r, []).append(i)
            for w in writes:
                lastw[w] = i
                readers[w] = []
        succs = [[] for _ in range(n_ops)]
        indeg = [len(p) for p in preds]
        for i, p in enumerate(preds):
            for q in p:
                succs[q].append(i)
        ready = set(i for i in range(n_ops) if indeg[i] == 0)
        order = []
        blev = [0.0] * n_ops
        for i in range(n_ops - 1, -1, -1):
            c = TL._cost(sem[i])[0]
            blev[i] = c + max([blev[j] for j in succs[i]], default=0.0)

        def vbs_of(rec):
            out = []
            for r in list(rec[2]) + list(rec[3]):
                if isinstance(r, tuple) and r[0] == "vb" and r[1] not in out:
                    out.append(r[1])
            return out
        remaining = {}
        for rec in sem:
            for v in vbs_of(rec):
                remaining[v] = remaining.get(v, 0) + 1
        holder = [None] * 8

        def bank_free(bk):
            return holder[bk] is None or remaining[holder[bk]] == 0

        def renamed(rec, extra=None):
            m = dict(bind)
            if extra:
                m.update(extra)
            ren = lambda r: (("pb", m[r[1]], r[2]) if (isinstance(r, tuple) and r[0] == "vb") else r)
            return (rec[0], rec[1], [ren(r) for r in rec[2]], [ren(r) for r in rec[3]], rec[4])

        def choose_banks(rec):
            new_v = [v for v in vbs_of(rec) if v not in bind]
            if not new_v:
                return {}
            free = [bk for bk in range(8) if bank_free(bk)]
            need_spare = 1 if any(vb_kind[v] == "proj" for v in new_v) else 0
            if len(free) < len(new_v) + need_spare:
                return None
            chosen = {}
            for v in new_v:
                bestb, bt = None, None
                for bk in free:
                    if bk in chosen.values():
                        continue
                    t_ = max([TL.w_t.get(("pb", bk, p_), (0.0, None))[0] for p_ in (0, 1)] +
                             [ft for p_ in (0, 1) for (ft, fe) in TL.r_t.get(("pb", bk, p_), ())] + [0.0])
                    if bt is None or t_ < bt:
                        bestb, bt = bk, t_
                chosen[v] = bestb
            return chosen

        while ready:
            best, bkey, bch = None, None, None
            for i in ready:
                ch = choose_banks(sem[i])
                if ch is None:
                    continue
                st = TL.start_time(renamed(sem[i], ch))
                if PRIO == "blevel":
                    key = (int(st // SCHED_Q), -blev[i], i)
                else:
                    key = (int(st // SCHED_Q), i)
                if bkey is None or key < bkey:
                    best, bkey, bch = i, key, ch
            if best is None:
                hk = [(vb_kind[holder[bk]], remaining[holder[bk]]) if holder[bk] is not None else None for bk in range(8)]
                rk = [[vb_kind[v] for v in vbs_of(sem[i]) if v not in bind] for i in ready]
                print("ALLOC DEADLOCK banks:", hk, "ready needs:", rk[:20], "n_ready", len(ready), "emitted", len(order))
            assert best is not None, "PSUM bank allocation deadlock in the list scheduler"
            picks = [best]
            if sem[best][4] is not None and sem[best][4] in GROUP_SEMS:
                picks += sorted(i for i in ready if i != best and sem[i][4] is sem[best][4])
            for bi in picks:
                ready.discard(bi)
                ch = bch if bi == best else choose_banks(sem[bi])
                for v, bk in (ch or {}).items():
                    bind[v] = bk
                    holder[bk] = v
                rec = renamed(sem[bi])
                for v in vbs_of(sem[bi]):
                    remaining[v] -= 1
                tl_end[0] = max(tl_end[0], TL.commit(rec))
                order.append(rec)
                for j in succs[bi]:
                    indeg[j] -= 1
                    if indeg[j] == 0:
                        ready.add(j)
        assert len(order) == n_ops
    else:
        order = tl_commit_list(list(setup_ops))
        order += tl_commit_list(gen_cpy())
        if PIPELINE:
            L12 = [("x", units[0]), ("x", units[1]) if len(units) > 1 else None, ("p", units[0], "A12"), ("w", 0)]
            L34 = []
            k = 0
            while k < len(units):
                ua = units[k]
                ub = units[k + 1] if k + 1 < len(units) else None
                un = units[k + 2] if k + 2 < len(units) else None
                un2 = units[k + 3] if k + 3 < len(units) else None
                L12 += [("p", ub, "A12"), ("p", ua, "B12")]
                if k == 0:
                    L12 += [("w", 1)]
                L12 += [("p", ub, "B12"), ("x", un), ("p", un, "A12"), ("x", un2)]
                L34 += [("p", ua, "A34"), ("p", ub, "A34"), ("p", ua, "B34"), ("p", ub, "B34")]
                k += 2
            L12 = [p for p in L12 if p is not None and p[1] is not None]
            L34 = [p for p in L34 if p[1] is not None]
            ops_of = {}
            prereq = {}
            slot_reader = {}
            for p in L12:
                if p[0] == "x":
                    ops_of[p] = gen_xload(p[1])
                    prereq[p] = [("p", p[1] - 2, "B34")]
                elif p[0] == "w":
                    ops_of[p] = late_weights(p[1])
                    prereq[p] = []
                else:
                    _, u, ph = p
                    ops_of[p] = gen_phase(u, ph)
                    pr = []
                    if ph == "A12":
                        pr.append(("p", u - 2, "A34"))
                    else:
                        pr += [("p", u, "A34"), ("p", u - 2, "B34")]
                    key = (u, 0 if ph == "A12" else 1)
                    if key in qs_of:
                        q_ = qs_of[key]
                        if q_ in slot_reader:
                            pr.append(slot_reader[q_])
                        slot_reader[q_] = ("p", u, "A34" if ph == "A12" else "B34")
                    prereq[p] = pr
            for p in L34:
                _, u, ph = p
                ops_of[p] = gen_phase(u, ph)
                prereq[p] = [("p", u, "A12" if ph == "A34" else "B12")]
            lanes = [L12, L34]
            pos = [0, 0]
            opi = [0, 0]
            done = set()
            allp = set(L12) | set(L34)

            def head(li):
                while pos[li] < len(lanes[li]):
                    p = lanes[li][pos[li]]
                    if opi[li] == 0 and any((q in allp) and (q not in done) for q in prereq[p]):
                        return None
                    if opi[li] < len(ops_of[p]):
                        return ops_of[p][opi[li]]
                    done.add(p)
                    pos[li] += 1
                    opi[li] = 0
                return None

            while True:
                progressed = True
                while progressed:
                    before = (tuple(pos), len(done))
                    h0, h1 = head(0), head(1)
                    progressed = (tuple(pos), len(done)) != before
                if h0 is None and h1 is None:
                    break
                if h1 is None:
                    pick = 0
                elif h0 is None:
                    pick = 1
                elif GREEDY:
                    pick = 0 if TL.start_time(h0) <= TL.start_time(h1) + LANE_BIAS else 1
                else:
                    pick = 0 if len(order) % 2 == 0 else 1
                rec = h0 if pick == 0 else h1
                opi[pick] += 1
                tl_end[0] = max(tl_end[0], TL.commit(rec))
                order.append(rec)
            assert pos[0] == len(L12) and pos[1] == len(L34), ("pipeline gating deadlock", pos, len(L12), len(L34))
        else:
            order += late_weights(0) + late_weights(1)
            for i_u, u in enumerate(units):
                if i_u == 0:
                    order += gen_xload(u)
                for ph in ("A12", "A34", "B12", "B34"):
                    order += gen_phase(u, ph)
                    if ph == "A12" and i_u + 1 < len(units):
                        order += gen_xload(units[i_u + 1])
    if _os0.environ.get("KTL"):
        print("TIMELINE predicted end: %.1f us; engine busy-until: %s" % (tl_end[0] / 1e3, {k: round(v / 1e3) for k, v in TL.busy.items()}))
    S.resolve(order)

    import os as _os
    _tr = int(_os.environ.get("KTRUNC", "0"))
    if _os.environ.get("KMARKS"):
        print("MARKS", marks, "total", len(S.ops))
        pass
    if _tr > 0:
        S.ops = S.ops[:_tr]
    S.finalize(engsems, group_sems=GROUP_SEMS,
               burst_sems=[dsems["xt"], dsems["ogt"]])
    finals = [(s, v) for s, v in S.final.items() if s in dsems.values()]
    with nc.Block() as block:
        @block.sync
        def _(e):
            S.emit("sp", e, extra_final=finals)

        @block.gpsimd
        def _(e):
            S.emit("pool", e)

        @block.scalar
        def _(e):
            S.emit("act", e)

        @block.vector
        def _(e):
            S.emit("dve", e)

        @block.tensor
        def _(e):
            S.emit("pe", e)
    for cm in reversed(ctx):
        cm.__exit__(None, None, None)
    return nc


_CACHE = {}


def _get_nc(nm, do_sample):
    key = (nm, do_sample)
    if key not in _CACHE:
        _CACHE[key] = build(nm, do_sample)
    return _CACHE[key]


def kernel(x_prompt, x_sample, cache_k_a, cache_v_a, cache_k_b, cache_v_b,
           t5_table, norm_a, w_in_a, q_norm_a, k_norm_a, sinks_a, w_out_a,
           norm_b, w_in_b, q_norm_b, k_norm_b, rel_bias_b, w_out_b, _nm=NM, _do_sample=True):
    f = lambda a: np.ascontiguousarray(np.asarray(a, dtype=np.float32))
    nm = _nm
    xp = f(x_prompt)[0]
    xsamp = f(x_sample)
    oha, ohb, ident, adm = _consts()
    rbT = np.zeros((384, 16), np.float32)
    rbT[:257] = f(rel_bias_b).T
    rbT = rbT.reshape(3, 128, 16)
    tok_per_core = 2048
    in_maps = []
    for c in range(NCORES):
        start = c * tok_per_core - NH * 128
        n = (NH + nm) * 128
        xin = np.zeros((n, D), np.float32)
        lo = max(start, 0)
        xin[lo - start:] = xp[lo:start + n]
        mask = np.full((128, 1), MASKV if c == 0 else 0.0, np.float32)
        in_maps.append({
            "xin": xin,
            "xs": xsamp[2 * c:2 * c + 2].reshape(128, D),
            "cka": f(cache_k_a)[2 * c:2 * c + 2].reshape(2, 128, 128),
            "cva": f(cache_v_a)[2 * c:2 * c + 2].reshape(2, 128, 128),
            "ckb": f(cache_k_b)[2 * c:2 * c + 2].reshape(2, 512, 1024),
            "cvb": f(cache_v_b)[2 * c:2 * c + 2].reshape(2, 512, 1024),
            "t5": f(t5_table), "rbT": rbT, "oha": oha, "ohb": ohb, "ident": ident, "adm": adm, "maskv": mask,
            "norm_a": f(norm_a), "norm_b": f(norm_b), "q_norm_a": f(q_norm_a), "k_norm_a": f(k_norm_a),
            "q_norm_b": f(q_norm_b), "k_norm_b": f(k_norm_b), "sinks": f(sinks_a),
            "w_in_a": f(w_in_a), "w_out_a": f(w_out_a), "w_in_b": f(w_in_b), "w_out_b": f(w_out_b),
        })
    nc = _get_nc(nm, _do_sample)
    res = run_bass_kernel_spmd(nc, in_maps, core_ids=list(range(NCORES)))
    R = res.results
    y_prompt = np.zeros((1, 16384, D), np.float32)
    for c in range(NCORES):
        y_prompt[0, c * tok_per_core:c * tok_per_core + nm * 128] = R[c]["y_p"]
    y_sample = np.stack([R[c]["y_s"].reshape(2, 64, D) for c in range(NCORES)]).reshape(16, 64, D)
    last = R[NCORES - 1]
    k_a_p = last["kap"].reshape(1, 128, 2, 64)
    v_a_p = last["vap"].reshape(1, 128, 2, 64)
    k_b_p = last["kbp"].reshape(1, 512, 16, 64)
    v_b_p = last["vbp"].reshape(1, 512, 16, 64)
    cat = lambda k, shp: np.concatenate([R[c][k] for c in range(NCORES)], axis=0).reshape(shp)
    k_a_s = cat("kas", (16, 128, 2, 64))
    v_a_s = cat("vas", (16, 128, 2, 64))
    k_b_s = cat("kbs", (16, 512, 16, 64))
    v_b_s = cat("vbs", (16, 512, 16, 64))
    return (y_prompt, y_sample, k_a_p, v_a_p, k_b_p, v_b_p, k_a_s, v_a_s, k_b_s, v_b_s)
```
